# Optimizing a Trainium2 kernel written in Bass

```python
import math
import jax, jax.numpy as jnp
from jax import lax
import numpy as np

D_MODEL = 2048
BATCH = 8
SEQ = 2048
DEPTH = 2

HEAD_DIM = 128
MIX_WIDTH = D_MODEL
GDN_HEADS = MIX_WIDTH // (2 * HEAD_DIM)
GDN_WIDTH = GDN_HEADS * HEAD_DIM
LRU_WIDTH = MIX_WIDTH - GDN_WIDTH
LRU_BLOCK = HEAD_DIM
LRU_BLOCKS = LRU_WIDTH // LRU_BLOCK
FOX_HEADS = MIX_WIDTH // (2 * HEAD_DIM)
FOX_WIDTH = FOX_HEADS * HEAD_DIM
MOBA_HEADS = (MIX_WIDTH - FOX_WIDTH) // HEAD_DIM
MOBA_WIDTH = MOBA_HEADS * HEAD_DIM
CONV_WIDTH = 4
GDN_CHUNK = 64
LRU_C = 8.0
FOX_Q_BLOCK = 128
MOBA_BLOCK = 256
MOBA_TOPK = 3
MOBA_Q_CHUNK = 16
ROPE_THETA = 500000.0
ROPE_DIM = HEAD_DIM // 4
D_FF = ((8 * D_MODEL + 3 * 256 - 1) // (3 * 256)) * 256
NORM_EPS = 1e-6
N_EVEN = (DEPTH + 1) // 2
N_ODD = DEPTH // 2
AB_SIZES = (GDN_WIDTH, GDN_WIDTH, GDN_WIDTH, GDN_WIDTH, GDN_HEADS, GDN_HEADS, LRU_WIDTH, LRU_WIDTH)
AB_IN = 4 * GDN_WIDTH + 2 * GDN_HEADS + 2 * LRU_WIDTH
CD_SIZES = (FOX_WIDTH, FOX_WIDTH, FOX_WIDTH, FOX_HEADS, MOBA_WIDTH, MOBA_WIDTH, MOBA_WIDTH)
CD_IN = 3 * FOX_WIDTH + FOX_HEADS + 3 * MOBA_WIDTH

kernel_name = 'hybrid_gdn_rglru_fox_moba_block'


def rmsnorm(x, w):
    xf = x.astype(jnp.float32)
    y = xf * lax.rsqrt(jnp.mean(xf * xf, axis=-1, keepdims=True) + NORM_EPS)
    return (y * w.astype(jnp.float32)).astype(x.dtype)


def l2norm(x):
    return x * lax.rsqrt(jnp.sum(x * x, axis=-1, keepdims=True) + NORM_EPS)


def split_cols(t, sizes):
    outs, start = [], 0
    for n in sizes:
        outs.append(t[..., start:start + n])
        start += n
    return outs


def to_heads(t, n_heads):
    b, s, _ = t.shape
    return t.reshape(b, s, n_heads, -1).transpose(0, 2, 1, 3)


def from_heads(t):
    b, h, s, d = t.shape
    return t.transpose(0, 2, 1, 3).reshape(b, s, h * d)


def causal_dwconv(x, w):
    c = x.shape[-1]
    return lax.conv_general_dilated(
        x, w[:, None, :].astype(x.dtype), window_strides=(1,),
        padding=[(w.shape[0] - 1, 0)], dimension_numbers=('NWC', 'WIO', 'NWC'),
        feature_group_count=c)


def partial_rotary(x):
    s = x.shape[2]
    half = ROPE_DIM // 2
    inv_freq = ROPE_THETA ** (-jnp.arange(half, dtype=jnp.float32) / half)
    ang = jnp.arange(s, dtype=jnp.float32)[:, None] * inv_freq[None, :]
    cos, sin = jnp.cos(ang), jnp.sin(ang)
    xf = x.astype(jnp.float32)
    x1, x2 = xf[..., :half], xf[..., half:ROPE_DIM]
    out = jnp.concatenate([x1 * cos - x2 * sin, x2 * cos + x1 * sin, xf[..., ROPE_DIM:]], axis=-1)
    return out.astype(x.dtype)


def gated_delta_rule_chunked(q, k, v, g, beta):
    b, h, s, dk = q.shape
    dv = v.shape[-1]
    c = GDN_CHUNK
    n = s // c
    q = (q * dk ** -0.5).reshape(b, h, n, c, dk)
    k = k.reshape(b, h, n, c, dk)
    v = v.reshape(b, h, n, c, dv)
    beta = beta.reshape(b, h, n, c)
    g = jnp.cumsum(g.reshape(b, h, n, c), axis=-1)
    incl = jnp.tril(jnp.ones((c, c), bool))
    strict = jnp.tril(jnp.ones((c, c), bool), -1)
    decay = jnp.where(incl, jnp.exp(jnp.where(incl, g[..., :, None] - g[..., None, :], 0.0)), 0.0)
    k_beta = k * beta[..., None]
    m = jnp.where(strict, jnp.einsum('bhncd,bhnsd->bhncs', k_beta, k) * decay, 0.0)
    eye = jnp.eye(c, dtype=m.dtype)
    t_inv = lax.linalg.triangular_solve(eye + m, jnp.broadcast_to(eye, m.shape),
                                        left_side=True, lower=True, unit_diagonal=True)
    u = jnp.einsum('bhncs,bhnse->bhnce', t_inv, v * beta[..., None])
    w = jnp.einsum('bhncs,bhnsd->bhncd', t_inv, k_beta * jnp.exp(g)[..., None])
    attn = jnp.where(incl, jnp.einsum('bhncd,bhnsd->bhncs', q, k) * decay, 0.0)
    q_dec = q * jnp.exp(g)[..., None]
    k_dec = k * jnp.exp(g[..., -1:] - g)[..., None]
    g_tot = jnp.exp(g[..., -1])

    def step(state, xs):
        u_i, w_i, q_i, a_i, k_i, gt_i = xs
        v_new = u_i - jnp.einsum('bhcd,bhde->bhce', w_i, state)
        o_i = jnp.einsum('bhcd,bhde->bhce', q_i, state) + jnp.einsum('bhcs,bhse->bhce', a_i, v_new)
        state = state * gt_i[..., None, None] + jnp.einsum('bhcd,bhce->bhde', k_i, v_new)
        return state, o_i

    xs = tuple(jnp.moveaxis(t, 2, 0) for t in (u, w, q_dec, attn, k_dec, g_tot))
    state0 = jnp.zeros((b, h, dk, dv), q.dtype)
    _, o = lax.scan(step, state0, xs)
    return jnp.moveaxis(o, 0, 2).reshape(b, h, s, dv)


def gated_deltanet(q, k, v, z, b_logit, a_logit, conv_w, a_log, dt_bias, out_norm_w):
    dtype = q.dtype
    bsz, s, _ = q.shape
    qkv = jax.nn.silu(causal_dwconv(jnp.concatenate([q, k, v], axis=-1), conv_w))
    q, k, v = split_cols(qkv, (GDN_WIDTH, GDN_WIDTH, GDN_WIDTH))
    q = l2norm(to_heads(q, GDN_HEADS).astype(jnp.float32))
    k = l2norm(to_heads(k, GDN_HEADS).astype(jnp.float32))
    v = to_heads(v, GDN_HEADS).astype(jnp.float32)
    beta = jax.nn.sigmoid(b_logit.astype(jnp.float32)).transpose(0, 2, 1)
    g = -(jnp.exp(a_log.astype(jnp.float32))
          * jax.nn.softplus(a_logit.astype(jnp.float32) + dt_bias.astype(jnp.float32)))
    g = g.transpose(0, 2, 1)
    o = gated_delta_rule_chunked(q, k, v, g, beta).transpose(0, 2, 1, 3)
    zh = z.reshape(bsz, s, GDN_HEADS, HEAD_DIM).astype(jnp.float32)
    o = rmsnorm(o, out_norm_w) * jax.nn.silu(zh)
    return o.reshape(bsz, s, GDN_WIDTH).astype(dtype)


def rglru_branch(xb, yb, conv_w, conv_b, wa, ba, wx, bx, lam):
    bsz, s, wdt = xb.shape
    xc = causal_dwconv(xb, conv_w) + conv_b
    xblk = xc.reshape(bsz, s, LRU_BLOCKS, LRU_BLOCK)
    r = jax.nn.sigmoid((jnp.einsum('bsnc,ncd->bsnd', xblk, wa).reshape(bsz, s, wdt) + ba).astype(jnp.float32))
    i = jax.nn.sigmoid((jnp.einsum('bsnc,ncd->bsnd', xblk, wx).reshape(bsz, s, wdt) + bx).astype(jnp.float32))
    log_a = -LRU_C * r * jax.nn.softplus(-lam.astype(jnp.float32))
    a = jnp.exp(log_a)
    inp = jnp.sqrt(-jnp.expm1(2.0 * log_a)) * i * xc.astype(jnp.float32)

    def combine(left, right):
        a_l, b_l = left
        a_r, b_r = right
        return a_l * a_r, a_r * b_l + b_r

    _, hseq = lax.associative_scan(combine, (a, inp), axis=1)
    return (hseq * jax.nn.gelu(yb.astype(jnp.float32))).astype(xb.dtype)


def forgetting_attention(q, k, v, log_f):
    b, h, s, d = q.shape
    nq = s // FOX_Q_BLOCK
    cum_f = jnp.cumsum(log_f, axis=-1)
    q_blocks = jnp.moveaxis(q.reshape(b, h, nq, FOX_Q_BLOCK, d), 2, 0)
    f_blocks = jnp.moveaxis(cum_f.reshape(b, h, nq, FOX_Q_BLOCK), 2, 0)
    k_pos = jnp.arange(s)
    scale = d ** -0.5

    def block(args):
        i, q_i, f_i = args
        sc = jnp.einsum('bhqd,bhkd->bhqk', q_i, k).astype(jnp.float32) * scale
        sc = sc + f_i[..., :, None] - cum_f[..., None, :]
        q_pos = i * FOX_Q_BLOCK + jnp.arange(FOX_Q_BLOCK)
        sc = jnp.where(k_pos[None, :] <= q_pos[:, None], sc, -jnp.inf)
        p = jax.nn.softmax(sc, axis=-1).astype(v.dtype)
        return jnp.einsum('bhqk,bhkd->bhqd', p, v)

    o = lax.map(block, (jnp.arange(nq), q_blocks, f_blocks))
    return jnp.moveaxis(o, 0, 2).reshape(b, h, s, d)


def moba_attention(q, k, v):
    b, h, s, d = q.shape
    nb = -(-s // MOBA_BLOCK)
    sp = nb * MOBA_BLOCK
    pad = ((0, 0), (0, 0), (0, sp - s), (0, 0))
    q, k, v = jnp.pad(q, pad), jnp.pad(k, pad), jnp.pad(v, pad)
    kb = k.reshape(b, h, nb, MOBA_BLOCK, d)
    vb = v.reshape(b, h, nb, MOBA_BLOCK, d)
    k_mean = jnp.mean(kb.astype(jnp.float32), axis=3)
    gate = jnp.einsum('bhtd,bhnd->bhtn', q.astype(jnp.float32), k_mean)
    t_blk = jnp.arange(sp) // MOBA_BLOCK
    past = jnp.arange(nb)[None, :] < t_blk[:, None]
    gate = jnp.where(past, gate, -jnp.inf)
    topk = min(MOBA_TOPK, nb)
    _, sel = lax.top_k(gate, topk)
    valid = jnp.arange(topk)[None, :] < t_blk[:, None]
    nc = sp // MOBA_Q_CHUNK
    q_c = jnp.moveaxis(q.reshape(b, h, nc, MOBA_Q_CHUNK, d), 2, 0)
    sel_c = jnp.moveaxis(sel.reshape(b, h, nc, MOBA_Q_CHUNK, topk), 2, 0)
    valid_c = valid.reshape(nc, MOBA_Q_CHUNK, topk)
    b_ix = jnp.arange(b)[:, None, None, None]
    h_ix = jnp.arange(h)[None, :, None, None]
    scale = d ** -0.5

    def chunk(args):
        i, q_i, sel_i, valid_i = args
        q_pos = i * MOBA_Q_CHUNK + jnp.arange(MOBA_Q_CHUNK)
        own = (i * MOBA_Q_CHUNK) // MOBA_BLOCK
        k_own = lax.dynamic_index_in_dim(kb, own, axis=2, keepdims=False)
        v_own = lax.dynamic_index_in_dim(vb, own, axis=2, keepdims=False)
        s_own = jnp.einsum('bhqd,bhkd->bhqk', q_i, k_own).astype(jnp.float32) * scale
        k_pos = own * MOBA_BLOCK + jnp.arange(MOBA_BLOCK)
        s_own = jnp.where(k_pos[None, :] <= q_pos[:, None], s_own, -jnp.inf)
        k_sel = kb[b_ix, h_ix, sel_i]
        v_sel = vb[b_ix, h_ix, sel_i]
        s_sel = jnp.einsum('bhqd,bhqnkd->bhqnk', q_i, k_sel).astype(jnp.float32) * scale
        s_sel = jnp.where(valid_i[None, None, :, :, None], s_sel, -jnp.inf)
        s_all = jnp.concatenate([s_own, s_sel.reshape(b, h, MOBA_Q_CHUNK, topk * MOBA_BLOCK)], axis=-1)
        p = jax.nn.softmax(s_all, axis=-1).astype(v.dtype)
        p_own = p[..., :MOBA_BLOCK]
        p_sel = p[..., MOBA_BLOCK:].reshape(b, h, MOBA_Q_CHUNK, topk, MOBA_BLOCK)
        return (jnp.einsum('bhqk,bhkd->bhqd', p_own, v_own)
                + jnp.einsum('bhqnk,bhqnkd->bhqd', p_sel, v_sel))

    o = lax.map(chunk, (jnp.arange(nc), q_c, sel_c, valid_c))
    return jnp.moveaxis(o, 0, 2).reshape(b, h, sp, d)[:, :, :s]


def gdn_rglru_mixer(hn, w_in, conv_qkv, a_log, dt_bias, out_norm_w,
                    lru_conv_w, lru_conv_b, lru_wa, lru_ba, lru_wx, lru_bx, lru_lambda, w_out):
    q, k, v, z, b_logit, a_logit, lx, ly = split_cols(hn @ w_in, AB_SIZES)
    o_a = gated_deltanet(q, k, v, z, b_logit, a_logit, conv_qkv, a_log, dt_bias, out_norm_w)
    o_b = rglru_branch(lx, ly, lru_conv_w, lru_conv_b, lru_wa, lru_ba, lru_wx, lru_bx, lru_lambda)
    return jnp.concatenate([o_a, o_b], axis=-1) @ w_out


def fox_moba_mixer(hn, w_in, f_bias, w_out):
    fq, fk, fv, ff, mq, mk, mv = split_cols(hn @ w_in, CD_SIZES)
    log_f = jax.nn.log_sigmoid(ff.astype(jnp.float32) + f_bias.astype(jnp.float32)).transpose(0, 2, 1)
    o_c = forgetting_attention(to_heads(fq, FOX_HEADS), to_heads(fk, FOX_HEADS), to_heads(fv, FOX_HEADS), log_f)
    o_d = moba_attention(partial_rotary(to_heads(mq, MOBA_HEADS)), partial_rotary(to_heads(mk, MOBA_HEADS)),
                         to_heads(mv, MOBA_HEADS))
    return jnp.concatenate([from_heads(o_c), from_heads(o_d)], axis=-1) @ w_out


def swiglu(hn, w_gate, w_up, w_down):
    return (jax.nn.silu(hn @ w_gate) * (hn @ w_up)) @ w_down


def setup_inputs(seed: int = 0) -> dict:
    key = jax.random.key(seed)
    ks = iter(jax.random.split(key, 32))
    f32 = jnp.float32

    def nrm(shape, scale):
        return jax.random.normal(next(ks), shape, f32) * scale

    x = nrm((BATCH, SEQ, D_MODEL), 1.0)
    ab_norm = 1.0 + nrm((N_EVEN, D_MODEL), 0.01)
    ab_w_in = nrm((N_EVEN, D_MODEL, AB_IN), D_MODEL ** -0.5)
    ab_conv_qkv = nrm((N_EVEN, CONV_WIDTH, 3 * GDN_WIDTH), CONV_WIDTH ** -0.5)
    ab_a_log = jnp.log(jax.random.uniform(next(ks), (N_EVEN, GDN_HEADS), f32, 1.0, 16.0))
    dt = jnp.exp(jax.random.uniform(next(ks), (N_EVEN, GDN_HEADS), f32, math.log(1e-3), math.log(1e-1)))
    ab_dt_bias = dt + jnp.log(-jnp.expm1(-dt))
    ab_out_norm = 1.0 + nrm((N_EVEN, HEAD_DIM), 0.01)
    ab_lru_conv_w = nrm((N_EVEN, CONV_WIDTH, LRU_WIDTH), CONV_WIDTH ** -0.5)
    ab_lru_conv_b = nrm((N_EVEN, LRU_WIDTH), 0.01)
    ab_lru_wa = nrm((N_EVEN, LRU_BLOCKS, LRU_BLOCK, LRU_BLOCK), LRU_BLOCK ** -0.5)
    ab_lru_ba = nrm((N_EVEN, LRU_WIDTH), 0.01)
    ab_lru_wx = nrm((N_EVEN, LRU_BLOCKS, LRU_BLOCK, LRU_BLOCK), LRU_BLOCK ** -0.5)
    ab_lru_bx = nrm((N_EVEN, LRU_WIDTH), 0.01)
    a0 = jax.random.uniform(next(ks), (N_EVEN, LRU_WIDTH), f32, 0.9, 0.999) ** (1.0 / LRU_C)
    ab_lru_lambda = jnp.log(a0) - jnp.log1p(-a0)
    ab_w_out = nrm((N_EVEN, MIX_WIDTH, D_MODEL), MIX_WIDTH ** -0.5)
    cd_norm = 1.0 + nrm((N_ODD, D_MODEL), 0.01)
    cd_w_in = nrm((N_ODD, D_MODEL, CD_IN), D_MODEL ** -0.5)
    cd_f_bias = jax.random.uniform(next(ks), (N_ODD, FOX_HEADS), f32, 1.0, 4.0)
    cd_w_out = nrm((N_ODD, MIX_WIDTH, D_MODEL), MIX_WIDTH ** -0.5)
    ffn_norm = 1.0 + nrm((DEPTH, D_MODEL), 0.01)
    ffn_w_gate = nrm((DEPTH, D_MODEL, D_FF), D_MODEL ** -0.5)
    ffn_w_up = nrm((DEPTH, D_MODEL, D_FF), D_MODEL ** -0.5)
    ffn_w_down = nrm((DEPTH, D_FF, D_MODEL), D_FF ** -0.5)
    final_norm = 1.0 + nrm((D_MODEL,), 0.01)
    return {'x': x, 'ab_norm': ab_norm, 'ab_w_in': ab_w_in, 'ab_conv_qkv': ab_conv_qkv,
            'ab_a_log': ab_a_log, 'ab_dt_bias': ab_dt_bias, 'ab_out_norm': ab_out_norm,
            'ab_lru_conv_w': ab_lru_conv_w, 'ab_lru_conv_b': ab_lru_conv_b,
            'ab_lru_wa': ab_lru_wa, 'ab_lru_ba': ab_lru_ba, 'ab_lru_wx': ab_lru_wx, 'ab_lru_bx': ab_lru_bx,
            'ab_lru_lambda': ab_lru_lambda, 'ab_w_out': ab_w_out,
            'cd_norm': cd_norm, 'cd_w_in': cd_w_in, 'cd_f_bias': cd_f_bias, 'cd_w_out': cd_w_out,
            'ffn_norm': ffn_norm, 'ffn_w_gate': ffn_w_gate, 'ffn_w_up': ffn_w_up, 'ffn_w_down': ffn_w_down,
            'final_norm': final_norm}


def reference(x, ab_norm, ab_w_in, ab_conv_qkv, ab_a_log, ab_dt_bias, ab_out_norm,
              ab_lru_conv_w, ab_lru_conv_b, ab_lru_wa, ab_lru_ba, ab_lru_wx, ab_lru_bx,
              ab_lru_lambda, ab_w_out, cd_norm, cd_w_in, cd_f_bias, cd_w_out,
              ffn_norm, ffn_w_gate, ffn_w_up, ffn_w_down, final_norm):
    h = x
    for layer in range(DEPTH):
        j = layer // 2
        if layer % 2 == 0:
            h = h + gdn_rglru_mixer(rmsnorm(h, ab_norm[j]), ab_w_in[j], ab_conv_qkv[j], ab_a_log[j],
                                    ab_dt_bias[j], ab_out_norm[j], ab_lru_conv_w[j], ab_lru_conv_b[j],
                                    ab_lru_wa[j], ab_lru_ba[j], ab_lru_wx[j], ab_lru_bx[j],
                                    ab_lru_lambda[j], ab_w_out[j])
        else:
            h = h + fox_moba_mixer(rmsnorm(h, cd_norm[j]), cd_w_in[j], cd_f_bias[j], cd_w_out[j])
        h = h + swiglu(rmsnorm(h, ffn_norm[layer]), ffn_w_gate[layer], ffn_w_up[layer], ffn_w_down[layer])
    return rmsnorm(h, final_norm)
```

```python
import os
import numpy as np
from contextlib import ExitStack
_os_env = os.environ
import concourse.bass as bass
import concourse.mybir as mybir
from concourse.bass_utils import run_bass_kernel_spmd

F32 = mybir.dt.float32
F32R = mybir.dt.float32r
BF16 = mybir.dt.bfloat16
ALU = mybir.AluOpType
AF = mybir.ActivationFunctionType
AX = mybir.AxisListType

S = 2048
D = 2048
DFF = 5632
NCH = 16
F0 = 6160
F1 = 6152
SCALE = 128.0 ** -0.5
GSTAG = int(_os_env.get("GSTAG", "0"))
ASTAG = int(_os_env.get("ASTAG", "0"))
NEG = -30000.0


import os as _os0
FORCE_INC = bool(_os0.environ.get("KINC"))


class Dep:
    __slots__ = ("w", "r")

    def __init__(self):
        self.w = None
        self.r = []


class FW:
    def __init__(self, nc, es, n_dma_sems=40):
        self.nc = nc
        self.eng = {"pe": nc.tensor, "dve": nc.vector, "act": nc.scalar, "pool": nc.gpsimd, "sp": nc.sync}
        self.sem = {k: es.enter_context(nc.semaphore("s_" + k)) for k in self.eng}
        self.cnt = {k: 0 for k in self.eng}
        self.waited = {k: {} for k in self.eng}
        self.dsem = [es.enter_context(nc.semaphore("d%d" % i)) for i in range(n_dma_sems)]
        self.dcnt = [0] * n_dma_sems
        self.dnext = 0

    def _wait(self, e, tok):
        if tok is None:
            return
        sem, val = tok
        if e == "pe" and sem is self.sem["pe"]:
            return
        key = id(sem)
        if self.waited[e].get(key, 0) >= val:
            return
        self.eng[e].wait_ge(sem, val)
        self.waited[e][key] = val

    def _deps(self, e, reads, writes):
        for b in reads:
            self._wait(e, b.w)
        for b in writes:
            self._wait(e, b.w)
            for t in b.r:
                self._wait(e, t)

    def _commit(self, tok, reads, writes):
        for b in writes:
            b.w = tok
            b.r = []
        for b in reads:
            if not any(b is w for w in writes):
                b.r.append(tok)
                if len(b.r) > 24:
                    b.r = b.r[-24:] if False else b.r

    def op(self, e, fn, reads=(), writes=(), inc=True):
        if FORCE_INC:
            inc = True
        self._deps(e, reads, writes)
        ins = fn(self.eng[e])
        if inc:
            self.cnt[e] += 1
            ins.then_inc(self.sem[e], 1)
            tok = (self.sem[e], self.cnt[e])
        else:
            tok = (self.sem[e], self.cnt[e] + 1)
        self._commit(tok, reads, writes)
        return tok

    def dma(self, out, in_, reads=(), writes=(), q="sp"):
        i = self.dnext
        self.dnext = (self.dnext + 1) % len(self.dsem)
        if self.dcnt[i] > 0:
            self._wait(q, (self.dsem[i], 16 * self.dcnt[i]))
        self._deps(q, reads, writes)
        self.eng[q].dma_start(out=out, in_=in_).then_inc(self.dsem[i], 16)
        self.dcnt[i] += 1
        tok = (self.dsem[i], 16 * self.dcnt[i])
        self._commit(tok, reads, writes)
        return tok

    def barrier(self):
        for e in self.eng:
            for o in self.eng:
                if o != e and self.cnt[o] > 0:
                    self._wait(e, (self.sem[o], self.cnt[o]))
            for i, s in enumerate(self.dsem):
                if self.dcnt[i] > 0:
                    self._wait(e, (s, 16 * self.dcnt[i]))


class T:
    def __init__(self, h):
        self.t = h
        self.d = Dep()

    def __getitem__(self, k):
        return self.t[k]


def build(debug=False, stop_after=None, lim=None):
    nc = bass.Bass("TRN2", target_bir_lowering=False)
    es = ExitStack()
    fw = FW(nc, es)

    def din(name, shape, dt=F32):
        return nc.dram_tensor(name, list(shape), dt, kind="ExternalInput").ap()

    def dscr(name, shape, dt=F32):
        return nc.dram_tensor(name, list(shape), dt, kind=("ExternalOutput" if debug else "Internal")).ap()

    xT = din("xT", [D, S])
    w_in0 = din("w_in0", [D, F0])
    w_in1 = din("w_in1", [D, F1])
    w_out0 = din("w_out0", [D, D])
    w_out1 = din("w_out1", [D, D])
    w_gate = din("w_gate", [2, D, DFF])
    w_up = din("w_up", [2, D, DFF])
    w_down = din("w_down", [2, DFF, D])
    normw = din("normw", [128, 5, NCH])
    convqkv = din("convqkv", [128, 24, 4])
    lruconv = din("lruconv", [128, 8, 4])
    lruvec = din("lruvec", [128, 4, 8])
    lruwa = din("lruwa", [128, 8, 128])
    lruwx = din("lruwx", [128, 8, 128])
    gdnvec = din("gdnvec", [8, 3])
    onw = din("onw", [64, 128])
    c_ident = din("c_ident", [128, 128])
    c_mask64 = din("c_mask64", [64, 3, 512])
    c_tri = din("c_tri", [128, 128])
    c_reset = din("c_reset", [8, S])
    c_selh = din("c_selh", [8, 8, 128])
    c_selp = din("c_selp", [128, 8, 128])
    c_sel3 = din("c_sel3", [128, 8, 128])
    c_rope = din("c_rope", [32, 2, S])
    c_moba = din("c_moba", [128, 3, 16, 8])
    outT = nc.dram_tensor("outT", [D, S], F32, kind="ExternalOutput").ap()
    HT = dscr("HT", [D, S])
    PT = dscr("PT", [F0, S])
    OT = dscr("OT", [D, S], BF16)

    uid = [0]

    def sb(name, shape, dt=F32, stack=None):
        uid[0] += 1
        return T((stack or es).enter_context(nc.sbuf_tensor("%s_%d" % (name, uid[0]), list(shape), dt)))

    ps = [T(es.enter_context(nc.psum_tensor("ps%d" % i, [128, 512], F32))) for i in range(8)]

    def mm(out, lhsT, rhs, start, stop, R, W, inc=True):
        return fw.op("pe", lambda e: e.matmul(out, lhsT, rhs, start=start, stop=stop), R, W, inc=inc)

    def tr(out, in_, ident, R, W, inc=True):
        return fw.op("pe", lambda e: e.transpose(out, in_, ident), R, W, inc=inc)

    def act(out, in_, func, R, W, **kw):
        return fw.op("act", lambda e: e.activation(out=out, in_=in_, func=func, **kw), R, W)

    def tt(out, in0, in1, op, R, W, eng="dve"):
        return fw.op(eng, lambda e: e.tensor_tensor(out=out, in0=in0, in1=in1, op=op), R, W)

    def ts(out, in0, s1, s2, op0, op1, R, W, eng="dve"):
        if s2 is None:
            return fw.op(eng, lambda e: e.tensor_scalar(out=out, in0=in0, scalar1=s1, scalar2=None, op0=op0), R, W)
        return fw.op(eng, lambda e: e.tensor_scalar(out=out, in0=in0, scalar1=s1, scalar2=s2, op0=op0, op1=op1), R, W)

    def stt(out, in0, scalar, in1, op0, op1, R, W):
        return fw.op("dve", lambda e: e.scalar_tensor_tensor(out=out, in0=in0, scalar=scalar, in1=in1, op0=op0, op1=op1), R, W)

    def cp(out, in_, R, W, eng="dve"):
        if eng == "act":
            return act(out, in_, AF.Copy, R, W)
        return fw.op(eng, lambda e: e.tensor_copy(out=out, in_=in_), R, W)

    def recip(out, in_, R, W):
        return fw.op("dve", lambda e: e.reciprocal(out=out, in_=in_), R, W)

    ident = sb("ident", [128, 128])
    ones_bf = sb("ones_bf", [128, 128], BF16)
    ones_r = sb("ones_r", [128, 128], F32R)
    epsc = sb("epsc", [128, 1])
    onec = sb("onec", [128, 1])
    nwt = sb("nwt", [128, 5, NCH])
    fw.dma(ident[:], c_ident, writes=[ident.d])
    fw.dma(nwt[:], normw, writes=[nwt.d])
    fw.op("dve", lambda e: e.memset(ones_bf[:], 1.0), [], [ones_bf.d])
    fw.op("dve", lambda e: e.memset(epsc[:], 1e-6), [], [epsc.d])
    fw.op("dve", lambda e: e.memset(onec[:], 1.0), [], [onec.d])
    ts(ones_r[:], ident[:], 0.0, 1.0, ALU.mult, ALU.add, [ident.d], [ones_r.d])

    HTd = [[Dep() for _ in range(2)] for _ in range(NCH)]
    PTd = Dep()
    OTd = Dep()
    PTf = {}

    evac_flip = [0]

    def run_rr(gens, tag=""):
        gens = list(gens)
        import os as _os
        if tag and tag in _os.environ.get("KSEQ", "").split(","):
            for g in gens:
                for _ in g:
                    pass
            return
        while gens:
            for g in list(gens):
                try:
                    next(g)
                except StopIteration:
                    gens.remove(g)

    class NS:
        pass

    def fast(gen, n):
        while True:
            for _ in range(n):
                try:
                    next(gen)
                except StopIteration:
                    return
            yield

    import os as _os1
    _ys = {1, 2} | set(range(7, 22))

    def YS(k):
        return k in _ys

    def norm_gen(th, nidx, hn, hbuf, sqb, rstd, src=HT, hbuf2=None):
        PF = 4
        for t2 in range(2):
            hb = hbuf if (t2 == 0 or hbuf2 is None) else hbuf2
            hd = hb.hd
            c0 = th * 1024 + t2 * 512
            pn = ps[6 + t2]

            def load(c):
                fw.dma(hb[:, c, :], src[c * 128:(c + 1) * 128, c0:c0 + 512], reads=[HTd[c][th]], writes=[hd[c]])

            for c in range(PF):
                load(c)
            for c in range(NCH):
                if c + PF < NCH:
                    load(c + PF)
                sq = sqb[c % 2]
                act(sq[:], hb[:, c, :], AF.Square, [hd[c]], [sq.d])
                mm(pn[:], ones_bf[:], sq[:], c == 0, c == NCH - 1, [ones_bf.d, sq.d], [pn.d])
                yield
            act(rstd[:], pn[:], AF.Ln, [pn.d, epsc.d], [rstd.d], scale=1.0 / D, bias=epsc[:])
            act(rstd[:], rstd[:], AF.Exp, [rstd.d], [rstd.d], scale=-0.5)
            for c in range(NCH):
                stt(hn[:, c, t2 * 512:(t2 + 1) * 512], hb[:, c, :], nwt[:, nidx, c:c + 1], rstd[:], ALU.mult, ALU.mult,
                    [hd[c], nwt.d, rstd.d], [hn.d])
                yield

    def norm_half(*a, **k):
        for _ in norm_gen(*a, **k):
            pass

    def proj_gen(W, Fdim, nk, actT, wring, consume, pre=None, order=None):
        Wv = W.rearrange("(k p) f -> p k f", p=128)
        nfc = (Fdim + 127) // 128
        if lim is not None:
            nfc = min(nfc, lim)
        seq = list(range(nfc)) if order is None else [f_ for f_ in order if f_ < nfc]
        for si, fc in enumerate(seq):
            m = min(128, Fdim - fc * 128)
            wb = wring[si % len(wring)]
            fw.dma(wb[:, 0:nk, 0:m], Wv[:, :, fc * 128:fc * 128 + m], writes=[wb.d], q="pool")
            for t2 in range(2):
                p = ps[(si % 3) * 2 + t2]
                if pre is not None:
                    pre(fc, m, t2)
                for k in range(nk):
                    mm(p[0:m, :], wb[:, k, 0:m], actT[:, k, t2 * 512:(t2 + 1) * 512], k == 0, k == nk - 1,
                       [wb.d, (actT.kd[k] if hasattr(actT, "kd") else actT.d)], [p.d], inc=(k == nk - 1))
                consume(fc, m, t2, p)
                yield

    def proj(*a, **k):
        for _ in proj_gen(*a, **k):
            pass

    def phase_inproj(Win, Fdim, nidx, src=HT, order1=None, side=None):
        with ExitStack() as ph:
            hns = [sb("hn%d" % i, [128, NCH, 1024], BF16, ph) for i in range(2)]
            hbuf = sb("hbuf", [128, NCH, 512], F32, ph)
            hbuf.hd = [Dep() for _ in range(NCH)]
            if side is None:
                hbuf2 = sb("hbufb", [128, NCH, 512], F32, ph)
                hbuf2.hd = [Dep() for _ in range(NCH)]
            else:
                hbuf2 = None
            sqb = [sb("sq%d" % i, [128, 512], BF16, ph) for i in range(2)]
            rstd = sb("rstd", [128, 512], F32, ph)
            wring = [sb("wr%d" % i, [128, NCH, 128], BF16, ph) for i in range(4)]
            stg = [sb("stg%d" % i, [128, 512], F32, ph) for i in range(4)]

            stgc = [0]

            def mkconsume(th):
                def consume(fc, m, t2, p):
                    st = stg[stgc[0] % 4]
                    stgc[0] += 1
                    evac_flip[0] ^= 1
                    cp(st[0:m, :], p[0:m, :], [p.d], [st.d], eng=("act" if evac_flip[0] else "dve"))
                    dd = Dep()
                    PTf[(fc, th, t2)] = dd
                    fw.dma(PT[fc * 128:fc * 128 + m, th * 1024 + t2 * 512: th * 1024 + (t2 + 1) * 512], st[0:m, :],
                           reads=[st.d], writes=[dd])
                return consume

            norm_half(0, nidx, hns[0], hbuf, sqb, rstd, src=src, hbuf2=hbuf2)
            if lim is None:
                run_rr([proj_gen(Win, Fdim, NCH, hns[0], wring, mkconsume(0)), norm_gen(1, nidx, hns[1], hbuf, sqb, rstd, src=src, hbuf2=hbuf2)])
                if side is None:
                    proj(Win, Fdim, NCH, hns[1], wring, mkconsume(1), order=order1)
                else:
                    run_rr([proj_gen(Win, Fdim, NCH, hns[1], wring, mkconsume(1), order=order1), side(ph)])
            else:
                proj(Win, Fdim, NCH, hns[0], wring, mkconsume(0))
            fw.barrier()

    def phase_outproj(Wout, src=HT):
        with ExitStack() as ph:
            ot = sb("ot", [128, NCH, 1024], BF16, ph)
            wring = [sb("wr%d" % i, [128, NCH, 128], BF16, ph) for i in range(4)]
            hst = [sb("hst%d" % i, [128, 512], F32, ph) for i in range(4)]
            for th in range(2):
                for c in range(NCH):
                    fw.dma(ot[:, c, :], OT[c * 128:(c + 1) * 128, th * 1024:(th + 1) * 1024], reads=[OTd], writes=[ot.d])

                def pre(fc, m, t2, th=th):
                    st = hst[(fc * 2 + t2) % 4]
                    c0 = th * 1024 + t2 * 512
                    fw.dma(st[:], src[fc * 128:(fc + 1) * 128, c0:c0 + 512], reads=[HTd[fc][th]], writes=[st.d])

                def consume(fc, m, t2, p, th=th):
                    st = hst[(fc * 2 + t2) % 4]
                    c0 = th * 1024 + t2 * 512
                    tt(st[:], p[:], st[:], ALU.add, [p.d, st.d], [st.d])
                    fw.dma(HT[fc * 128:(fc + 1) * 128, c0:c0 + 512], st[:], reads=[st.d], writes=[HTd[fc][th]])

                proj(Wout, D, NCH, ot, wring, consume, pre=pre)
            fw.barrier()

    def phase_ffn(layer, nidx, Wout=None, res_src=HT, fin=False):
        with ExitStack() as ph:
            hn = sb("hn", [128, NCH, 1024], BF16, ph)
            hbuf = sb("hbuf", [128, NCH, 512], F32, ph)
            hbuf.hd = [Dep() for _ in range(NCH)]
            sqb = [sb("sq%d" % i, [128, 512], BF16, ph) for i in range(2)]
            rstd = sb("rstd", [128, 512], F32, ph)
            aT = sb("aT", [128, 44, 1024], BF16, ph)
            wg = [sb("wg%d" % i, [128, NCH, 128], BF16, ph) for i in range(2)]
            wu = [sb("wu%d" % i, [128, NCH, 128], BF16, ph) for i in range(2)]
            wd = [sb("wd%d" % i, [128, 44, 128], BF16, ph) for i in range(2)]
            sg = [sb("sg%d" % i, [128, 512], F32, ph) for i in range(2)]
            hst = [sb("hst%d" % i, [128, 512], F32, ph) for i in range(4)]
            Wg = w_gate[layer].rearrange("(k p) f -> p k f", p=128)
            Wu = w_up[layer].rearrange("(k p) f -> p k f", p=128)
            def gateup(th):
                for fc in range(44):
                    g = wg[fc % 2]
                    u = wu[fc % 2]
                    fw.dma(g[:], Wg[:, :, fc * 128:(fc + 1) * 128], writes=[g.d], q="pool")
                    fw.dma(u[:], Wu[:, :, fc * 128:(fc + 1) * 128], writes=[u.d], q="pool")
                    for t2 in range(2):
                        pg = ps[(fc % 2) * 4 + t2 * 2]
                        pu = ps[(fc % 2) * 4 + t2 * 2 + 1]
                        for k in range(NCH):
                            mm(pg[:], g[:, k, :], hn[:, k, t2 * 512:(t2 + 1) * 512], k == 0, k == NCH - 1, [g.d, hn.d], [pg.d],
                               inc=(k == NCH - 1))
                        for k in range(NCH):
                            mm(pu[:], u[:, k, :], hn[:, k, t2 * 512:(t2 + 1) * 512], k == 0, k == NCH - 1, [u.d, hn.d], [pu.d],
                               inc=(k == NCH - 1))
                        s_ = sg[t2]
                        act(s_[:], pg[:], AF.Silu, [pg.d], [s_.d])
                        tt(aT[:, fc, t2 * 512:(t2 + 1) * 512], s_[:], pu[:], ALU.mult, [s_.d, pu.d], [aT.d])

            def mkpre(th):
                def pre(fc, m, t2):
                    st = hst[(fc * 2 + t2) % 4]
                    c0 = th * 1024 + t2 * 512
                    fw.dma(st[:], HT[fc * 128:(fc + 1) * 128, c0:c0 + 512], reads=[HTd[fc][th]], writes=[st.d])
                return pre

            def mkconsume(th):
                def consume(fc, m, t2, p):
                    st = hst[(fc * 2 + t2) % 4]
                    c0 = th * 1024 + t2 * 512
                    tt(st[:], p[:], st[:], ALU.add, [p.d, st.d], [st.d])
                    fw.dma(HT[fc * 128:(fc + 1) * 128, c0:c0 + 512], st[:], reads=[st.d], writes=[HTd[fc][th]])
                return consume

            if Wout is None or lim is not None:
                norm_half(0, nidx, hn, hbuf, sqb, rstd)
            else:
                class V:
                    def __init__(self, ap, d):
                        self.t = ap
                        self.d = d

                    def __getitem__(self, k):
                        return self.t[k]

                ots = []
                for th_ in range(2):
                    v = V(aT[:, th_ * NCH:(th_ + 1) * NCH, :], aT.d)
                    v.kd = [Dep() for _ in range(NCH)]
                    ots.append(v)
                owring = wg + wu

                def oload(th):
                    for c in range(NCH):
                        fw.dma(ots[th][:, c, :], OT[c * 128:(c + 1) * 128, th * 1024:(th + 1) * 1024], reads=[OTd], writes=[ots[th].kd[c]])

                def mkopre(th):
                    def pre(fc, m, t2):
                        st = hst[(fc * 2 + t2) % 4]
                        c0 = th * 1024 + t2 * 512
                        fw.dma(st[:], res_src[fc * 128:(fc + 1) * 128, c0:c0 + 512], reads=[HTd[fc][th]], writes=[st.d])
                    return pre

                def mkocons(th):
                    def consume(fc, m, t2, p):
                        st = hst[(fc * 2 + t2) % 4]
                        c0 = th * 1024 + t2 * 512
                        tt(st[:], p[:], st[:], ALU.add, [p.d, st.d], [st.d])
                        fw.dma(HT[fc * 128:(fc + 1) * 128, c0:c0 + 512], st[:], reads=[st.d], writes=[HTd[fc][th]])
                    return consume

                oload(0)
                oload(1)
                proj(Wout, D, NCH, ots[0], owring, mkocons(0), pre=mkopre(0))
                run_rr([proj_gen(Wout, D, NCH, ots[1], owring, mkocons(1), pre=mkopre(1)), fast(norm_gen(0, nidx, hn, hbuf, sqb, rstd), 2)])
                fw.op("pe", lambda e: e.matmul(ps[0][0:1, 0:2], ones_bf[0:1, 0:1], ones_bf[0:1, 0:2], start=True, stop=True),
                      [ones_bf.d] + ots[0].kd + ots[1].kd + [aT.d], [ps[0].d])
            gateup(0)
            run_rr([proj_gen(w_down[layer], D, 44, aT, wd, mkconsume(0), pre=mkpre(0)), fast(norm_gen(1, nidx, hn, hbuf, sqb, rstd), 2)])
            gateup(1)
            if not fin:
                proj(w_down[layer], D, 44, aT, wd, mkconsume(1), pre=mkpre(1))
            else:
                ost = sg

                def final_gen(th):
                    PF = 4
                    hd = hbuf.hd
                    for t2 in range(2):
                        c0 = th * 1024 + t2 * 512
                        pn = ps[6 + t2]

                        def load(c):
                            fw.dma(hbuf[:, c, :], HT[c * 128:(c + 1) * 128, c0:c0 + 512], reads=[HTd[c][th]], writes=[hd[c]])

                        for c in range(PF):
                            load(c)
                        for c in range(NCH):
                            if c + PF < NCH:
                                load(c + PF)
                            sq = sqb[c % 2]
                            act(sq[:], hbuf[:, c, :], AF.Square, [hd[c]], [sq.d])
                            mm(pn[:], ones_bf[:], sq[:], c == 0, c == NCH - 1, [ones_bf.d, sq.d], [pn.d])
                            yield
                        act(rstd[:], pn[:], AF.Ln, [pn.d, epsc.d], [rstd.d], scale=1.0 / D, bias=epsc[:])
                        act(rstd[:], rstd[:], AF.Exp, [rstd.d], [rstd.d], scale=-0.5)
                        for c in range(NCH):
                            o = ost[c % 2]
                            stt(o[:], hbuf[:, c, :], nwt[:, 4, c:c + 1], rstd[:], ALU.mult, ALU.mult, [hd[c], nwt.d, rstd.d], [o.d])
                            fw.dma(outT[c * 128:(c + 1) * 128, c0:c0 + 512], o[:], reads=[o.d], writes=[Dep()])
                            yield

                run_rr([proj_gen(w_down[layer], D, 44, aT, wd, mkconsume(1), pre=mkpre(1)), fast(final_gen(0), 2)])
                for _ in final_gen(1):
                    pass
            fw.barrier()

    def phase_final():
        with ExitStack() as ph:
            hbuf = sb("hbuf", [128, NCH, 512], F32, ph)
            sqb = [sb("sq%d" % i, [128, 512], BF16, ph) for i in range(2)]
            rstd = sb("rstd", [128, 512], F32, ph)
            ost = [sb("ost%d" % i, [128, 512], F32, ph) for i in range(4)]
            od = Dep()
            for th in range(2):
                for t2 in range(2):
                    c0 = th * 1024 + t2 * 512
                    pn = ps[6 + t2]
                    for c in range(NCH):
                        fw.dma(hbuf[:, c, :], HT[c * 128:(c + 1) * 128, c0:c0 + 512], reads=[HTd[c][th]], writes=[hbuf.d])
                        sq = sqb[c % 2]
                        act(sq[:], hbuf[:, c, :], AF.Square, [hbuf.d], [sq.d])
                        mm(pn[:], ones_bf[:], sq[:], c == 0, c == NCH - 1, [ones_bf.d, sq.d], [pn.d])
                    act(rstd[:], pn[:], AF.Sqrt, [pn.d, epsc.d], [rstd.d], scale=1.0 / D, bias=epsc[:])
                    recip(rstd[:], rstd[:], [rstd.d], [rstd.d])
                    for c in range(NCH):
                        o = ost[c % 4]
                        stt(o[:], hbuf[:, c, :], nwt[:, 4, c:c + 1], rstd[:], ALU.mult, ALU.mult, [hbuf.d, nwt.d, rstd.d], [o.d])
                        fw.dma(outT[c * 128:(c + 1) * 128, c0:c0 + 512], o[:], reads=[o.d], writes=[od])
            fw.barrier()


    def lru_side(ph):
        lcw = sb("lcw", [128, 8, 4], F32, ph)
        lv = sb("lv", [128, 4, 8], F32, ph)
        wa_f = sb("wa_f", [128, 8, 128], F32, ph)
        wx_f = sb("wx_f", [128, 8, 128], F32, ph)
        wa_r = sb("wa_r", [128, 8, 128], F32R, ph)
        wx_r = sb("wx_r", [128, 8, 128], F32R, ph)
        cst = sb("cst", [128, 8], F32, ph)
        cst2 = sb("cst2", [128, 8], F32, ph)
        fw.dma(lcw[:], lruconv, writes=[lcw.d])
        fw.dma(lv[:], lruvec, writes=[lv.d])
        fw.dma(wa_f[:], lruwa, writes=[wa_f.d])
        fw.dma(wx_f[:], lruwx, writes=[wx_f.d])
        cp(wa_r[:], wa_f[:], [wa_f.d], [wa_r.d], eng="act")
        cp(wx_r[:], wx_f[:], [wx_f.d], [wx_r.d], eng="act")
        act(cst[:], lv[:, 3, :], AF.Exp, [lv.d], [cst.d], scale=-1.0)
        act(cst[:], cst[:], AF.Ln, [cst.d, onec.d], [cst.d], bias=onec[:])
        ts(cst2[:], cst[:], -16.0, None, ALU.mult, None, [cst.d], [cst2.d])
        ts(cst[:], cst[:], -8.0, None, ALU.mult, None, [cst.d], [cst.d])
        B = [ps[6], ps[7], ps[6], ps[7]]
        xraw = sb("lxraw", [128, 3 + S], F32, ph)
        xc = sb("xc", [128, S], F32R, ph)
        yb = sb("yb", [128, S], F32, ph)
        r = sb("r", [128, S], F32, ph)
        ii = sb("ii", [128, S], F32, ph)
        aa = sb("aa", [128, S], F32, ph)
        t1 = sb("t1", [128, S], F32, ph)
        ho = sb("ho", [128, S], BF16, ph)
        fw.op("dve", lambda e: e.memset(xraw[:, 0:3], 0.0), [], [xraw.d])

        def lru(nb):
            dx = [PTf[(32 + nb, th_, t_)] for th_ in range(2) for t_ in range(2)]
            dy = [PTf[(40 + nb, th_, t_)] for th_ in range(2) for t_ in range(2)]
            fw.dma(xraw[:, 3:3 + S], PT[4096 + nb * 128:4096 + (nb + 1) * 128, :], reads=dx, writes=[xraw.d])
            fw.dma(yb[:], PT[5120 + nb * 128:5120 + (nb + 1) * 128, :], reads=dy, writes=[yb.d])
            ts(t1[:], xraw[:, 0:S], lcw[:, nb, 0:1], lv[:, 0, nb:nb + 1], ALU.mult, ALU.add, [xraw.d, lcw.d, lv.d], [t1.d])
            yield
            for j in range(1, 3):
                stt(t1[:], xraw[:, j:j + S], lcw[:, nb, j:j + 1], t1[:], ALU.mult, ALU.add, [xraw.d, lcw.d, t1.d], [t1.d])
                yield
            stt(xc[:], xraw[:, 3:3 + S], lcw[:, nb, 3:4], t1[:], ALU.mult, ALU.add, [xraw.d, lcw.d, t1.d], [xc.d])
            yield
            for t4 in range(4):
                sl = slice(t4 * 512, (t4 + 1) * 512)
                pa = B[(t4 % 2) * 2]
                pb = B[(t4 % 2) * 2 + 1]
                mm(pa[:], wa_r[:, nb, :], xc[:, sl], True, True, [wa_r.d, xc.d], [pa.d])
                mm(pb[:], wx_r[:, nb, :], xc[:, sl], True, True, [wx_r.d, xc.d], [pb.d])
                act(r[:, sl], pa[:], AF.Sigmoid, [pa.d, lv.d], [r.d], bias=lv[:, 1, nb:nb + 1])
                act(ii[:, sl], pb[:], AF.Sigmoid, [pb.d, lv.d], [ii.d], bias=lv[:, 2, nb:nb + 1])
                yield
            act(aa[:], r[:], AF.Exp, [r.d, cst.d], [aa.d], scale=cst[:, nb:nb + 1])
            act(t1[:], r[:], AF.Exp, [r.d, cst2.d], [t1.d], scale=cst2[:, nb:nb + 1])
            yield
            ts(t1[:], t1[:], -1.0, 1.0, ALU.mult, ALU.add, [t1.d], [t1.d])
            act(t1[:], t1[:], AF.Sqrt, [t1.d], [t1.d])
            yield
            tt(t1[:], t1[:], ii[:], ALU.mult, [t1.d, ii.d], [t1.d], eng="pool")
            yield
            tt(t1[:], t1[:], xc[:].bitcast(F32), ALU.mult, [t1.d, xc.d], [t1.d], eng="pool")
            yield
            fw.op("dve", lambda e: e.tensor_tensor_scan(out=r[:], data0=aa[:], data1=t1[:], initial=0.0, op0=ALU.mult,
                                                        op1=ALU.add), [aa.d, t1.d], [r.d])
            act(ii[:], yb[:], AF.Square, [yb.d], [ii.d])
            yield
            ts(ii[:], ii[:], 0.044715, 1.0, ALU.mult, ALU.add, [ii.d], [ii.d])
            yield
            tt(ii[:], ii[:], yb[:], ALU.mult, [ii.d, yb.d], [ii.d])
            act(ii[:], ii[:], AF.Sigmoid, [ii.d], [ii.d], scale=1.5957691216)
            yield
            tt(ii[:], ii[:], yb[:], ALU.mult, [ii.d, yb.d], [ii.d])
            yield
            tt(ho[:], ii[:], r[:], ALU.mult, [ii.d, r.d], [ho.d])
            fw.dma(OT[1024 + nb * 128:1024 + (nb + 1) * 128, :], ho[:], reads=[ho.d], writes=[Dep()])
            yield

        def stream():
            for nb in range(8):
                while (40 + nb, 1, 1) not in PTf or (32 + nb, 1, 1) not in PTf:
                    yield
                yield
                yield
                k = 0
                for _ in lru(nb):
                    k += 1
                    if k % 2 == 0:
                        yield
                yield

        return stream()

    def phase_mix0():
        with ExitStack() as ph:
            m64 = sb("m64", [64, 3, 512], F32, ph)
            selh = sb("selh", [8, 8, 128], F32, ph)
            gv = sb("gv", [8, 3], F32, ph)
            onwt = sb("onwt", [64, 128], F32, ph)
            cw = sb("cw", [128, 24, 4], F32, ph)
            betar = sb("betar", [8, S], F32, ph)
            gcr = sb("gcr", [8, S], F32, ph)
            gccol = sb("gccol", [64, 32, 8], F32, ph)
            bcol = sb("bcol", [64, 32, 8], F32, ph)
            egccol = sb("egccol", [64, 32, 8], F32, ph)
            Sts = [sb("St%d" % h, [128, 128], F32R, ph) for h in range(8)]
            fw.dma(m64[:], c_mask64, writes=[m64.d])
            fw.dma(selh[:], c_selh, writes=[selh.d])
            fw.dma(gv[:], gdnvec, writes=[gv.d])
            fw.dma(onwt[:], onw, writes=[onwt.d])
            fw.dma(cw[:], convqkv, writes=[cw.d])
            maskI = m64[:, 0, :]
            negS = m64[:, 1, :]
            idb = m64[:, 2, :]
            with ExitStack() as pre:
                reset = sb("reset", [8, S], F32, pre)
                braw = sb("braw", [8, S], F32, pre)
                araw = sb("araw", [8, S], F32, pre)
                negA = sb("negA", [8, 1], F32, pre)
                fw.dma(reset[:], c_reset, writes=[reset.d])
                fw.dma(braw[:], PT[6144:6152, :], reads=[PTd], writes=[braw.d])
                fw.dma(araw[:], PT[6152:6160, :], reads=[PTd], writes=[araw.d])
                act(betar[:], braw[:], AF.Sigmoid, [braw.d], [betar.d])
                act(araw[:], araw[:], AF.Exp, [araw.d, gv.d], [araw.d], bias=gv[:, 1:2])
                act(araw[:], araw[:], AF.Ln, [araw.d, onec.d], [araw.d], bias=onec[0:8, :])
                act(negA[:], gv[:, 0:1], AF.Exp, [gv.d], [negA.d])
                ts(negA[:], negA[:], -1.0, None, ALU.mult, None, [negA.d], [negA.d])
                ts(araw[:], araw[:], negA[:, 0:1], None, ALU.mult, None, [araw.d, negA.d], [araw.d])
                fw.op("dve", lambda e: e.tensor_tensor_scan(out=gcr[:], data0=reset[:], data1=araw[:], initial=0.0, op0=ALU.mult,
                                                            op1=ALU.add), [reset.d, araw.d], [gcr.d])
                for src, dst in ((gcr, gccol), (betar, bcol)):
                    p = ps[0]
                    for c in range(32):
                        tr(p[0:64, c * 8:(c + 1) * 8], src[:, c * 64:(c + 1) * 64], ident[0:8, 0:8], [src.d, ident.d], [p.d], inc=(c == 31))
                    cp(dst[:].rearrange("p a b -> p (a b)"), p[0:64, 0:256], [p.d], [dst.d])
                act(egccol[:], gccol[:], AF.Exp, [gccol.d], [egccol.d])
                for h in range(8):
                    ts(Sts[h][:], ident[:], 0.0, None, ALU.mult, None, [ident.d, Sts[h].d], [Sts[h].d])
                fw.barrier()

            def mkslot(g):
                n = NS()
                n.B = [ps[4 * g + i] for i in range(4)]
                n.xraw = sb("xraw", [128, 3 + 512], F32, ph)
                n.acc = sb("acc", [128, 512], F32, ph)
                n.vs = sb("vs", [128, 512], F32, ph)
                n.qT = sb("qT", [128, 512], F32R, ph)
                n.kT = sb("kT", [128, 512], F32R, ph)
                n.vtok = sb("vtok", [64, 8, 128], F32, ph)
                n.ktok = sb("ktok", [64, 8, 128], F32, ph)
                n.sqr = sb("sqr", [128, 512], F32R, ph)
                n.rn = sb("rn", [128, 512], F32, ph)
                n.gcb = sb("gcb", [128, 512], F32, ph)
                n.egcb = sb("egcb", [128, 512], F32, ph)
                n.betab = sb("betab", [64, 512], F32, ph)
                n.Dt = sb("Dt", [64, 512], F32, ph)
                n.tmpa = sb("tmpa", [64, 512], F32, ph)
                n.tmpb = sb("tmpb", [64, 512], F32, ph)
                n.tmpk = sb("tmpk", [64, 8, 128], F32, ph)
                n.attnT = sb("attnT", [64, 512], F32R, ph)
                n.A = [sb("A%d" % i, [64, 512], F32R, ph) for i in range(2)]
                n.At = [sb("At%d" % i, [64, 512], F32R, ph) for i in range(2)]
                n.X = sb("X", [64, 512], F32R, ph)
                n.vb = sb("vb", [64, 8, 128], F32R, ph)
                n.kbg = sb("kbg", [64, 8, 128], F32R, ph)
                n.kdec = sb("kdec", [64, 8, 128], F32R, ph)
                n.kds = sb("kds", [64, 8], F32, ph)
                n.u = sb("u", [64, 8, 128], F32, ph)
                n.wT = sb("wT", [128, 512], F32R, ph)
                n.qdT = sb("qdT", [128, 512], F32R, ph)
                n.vnew = sb("vnew", [64, 128], F32R, ph)
                n.ob = sb("ob", [64, 8, 128], F32, ph)
                n.oss = sb("oss", [64, 8], F32, ph)
                n.zr = sb("zr", [128, 512], F32, ph)
                n.oo = sb("oo", [128, 512], BF16, ph)
                return n

            slots = [mkslot(0), mkslot(1)]

            def gdn(h, b, n):
                B0, B1, B2, B3 = n.B
                St = Sts[h]
                bs = slice(b * 512, (b + 1) * 512)
                xraw, acc, vs, qT, kT = n.xraw, n.acc, n.vs, n.qT, n.kT

                def conv_silu(row0, wchunk):
                    if b == 0:
                        fw.op("dve", lambda e: e.memset(xraw[:, 0:3], 0.0), [], [xraw.d])
                        fw.dma(xraw[:, 3:515], PT[row0:row0 + 128, 0:512], reads=[PTd], writes=[xraw.d])
                    else:
                        fw.dma(xraw[:, 0:515], PT[row0:row0 + 128, b * 512 - 3:b * 512 + 512], reads=[PTd], writes=[xraw.d])
                    ts(acc[:], xraw[:, 0:512], cw[:, wchunk, 0:1], None, ALU.mult, None, [xraw.d, cw.d], [acc.d])
                    for j in range(1, 4):
                        stt(acc[:], xraw[:, j:j + 512], cw[:, wchunk, j:j + 1], acc[:], ALU.mult, ALU.add, [xraw.d, cw.d, acc.d], [acc.d])
                    act(vs[:], acc[:], AF.Silu, [acc.d], [vs.d])

                def l2n(dst, extra_scale):
                    act(n.sqr[:], vs[:], AF.Square, [vs.d], [n.sqr.d])
                    mm(B0[:], ones_r[:], n.sqr[:], True, True, [ones_r.d, n.sqr.d], [B0.d])
                    act(n.rn[:], B0[:], AF.Ln, [B0.d, epsc.d], [n.rn.d], bias=epsc[:], scale=1.0)
                    act(n.rn[:], n.rn[:], AF.Exp, [n.rn.d], [n.rn.d], scale=-0.5)
                    stt(dst[:], vs[:], extra_scale, n.rn[:], ALU.mult, ALU.mult, [vs.d, n.rn.d], [dst.d])

                def to_tok(src_fn, src_dep, dst):
                    for g2 in range(2):
                        p = B1 if g2 == 0 else B2
                        for j in range(4):
                            c = g2 * 4 + j
                            tr(p[0:64, j * 128:(j + 1) * 128], src_fn(c), ident[:], [src_dep, ident.d], [p.d], inc=(j == 3))
                        cp(dst[:, g2 * 4:(g2 + 1) * 4, :].rearrange("p a b -> p (a b)"), p[0:64, :], [p.d], [dst.d],
                           eng=("act" if g2 else "dve"))

                conv_silu(h * 128, h)
                if YS(1):
                    yield
                l2n(qT, SCALE)
                if YS(2):
                    yield
                conv_silu(1024 + h * 128, 8 + h)
                if YS(3):
                    yield
                l2n(kT, 1.0)
                to_tok(lambda c: kT[:, c * 64:(c + 1) * 64].bitcast(F32), kT.d, n.ktok)
                if YS(4):
                    yield
                conv_silu(2048 + h * 128, 16 + h)
                to_tok(lambda c: vs[:, c * 64:(c + 1) * 64], vs.d, n.vtok)
                if YS(5):
                    yield
                gcb, egcb, betab, Dt = n.gcb, n.egcb, n.betab, n.Dt
                mm(B0[:], selh[:, h, :], gcr[:, bs], True, True, [selh.d, gcr.d], [B0.d])
                cp(gcb[:], B0[:], [B0.d], [gcb.d])
                act(egcb[:], B0[:], AF.Exp, [B0.d], [egcb.d])
                mm(B1[:], selh[:, h, :], betar[:, bs], True, True, [selh.d, betar.d], [B1.d])
                cp(betab[:], B1[0:64, :], [B1.d], [betab.d], eng="act")
                if YS(6):
                    yield
                gcc = gccol[:, b * 8:(b + 1) * 8, h]
                bcc = bcol[:, b * 8:(b + 1) * 8, h]
                egc = egccol[:, b * 8:(b + 1) * 8, h]
                tt(Dt[:].rearrange("p (a b) -> p a b", b=64), gcb[0:64, :].rearrange("p (a b) -> p a b", b=64),
                   gcc.unsqueeze(2).broadcast_to([64, 8, 64]), ALU.subtract, [gcb.d, gccol.d], [Dt.d])
                ts(Dt[:], Dt[:], 0.0, None, ALU.min, None, [Dt.d], [Dt.d])
                act(Dt[:], Dt[:], AF.Exp, [Dt.d], [Dt.d])
                pk, pq = B2, B3
                for j in range(8):
                    cs = slice(j * 64, (j + 1) * 64)
                    mm(pk[0:64, cs], kT[:, cs], kT[:, cs], True, True, [kT.d], [pk.d], inc=(j == 7))
                for j in range(8):
                    cs = slice(j * 64, (j + 1) * 64)
                    mm(pq[0:64, cs], kT[:, cs], qT[:, cs], True, True, [kT.d, qT.d], [pq.d], inc=(j == 7))
                if YS(7):
                    yield
                tt(n.tmpa[:], pq[0:64, :], Dt[:], ALU.mult, [pq.d, Dt.d], [n.tmpa.d])
                tt(n.attnT[:], n.tmpa[:], maskI, ALU.mult, [n.tmpa.d, m64.d], [n.attnT.d])
                a0 = n.A[0]
                tt(n.tmpb[:], pk[0:64, :], Dt[:], ALU.mult, [pk.d, Dt.d], [n.tmpb.d])
                tt(n.tmpb[:], n.tmpb[:], betab[:], ALU.mult, [n.tmpb.d, betab.d], [n.tmpb.d])
                tt(a0[:], n.tmpb[:], negS, ALU.mult, [n.tmpb.d, m64.d], [a0.d])
                if YS(8):
                    yield
                for j in range(8):
                    tr(B0[0:64, j * 64:(j + 1) * 64], a0[:, j * 64:(j + 1) * 64].bitcast(F32), ident[0:64, 0:64], [a0.d, ident.d], [B0.d], inc=(j == 7))
                cp(n.At[0][:], B0[0:64, :], [B0.d], [n.At[0].d], eng="act")
                X = n.X
                tt(X[:], a0[:].bitcast(F32), idb, ALU.add, [a0.d, m64.d], [X.d])
                if YS(9):
                    yield
                cur = 0
                for l in range(1, 6):
                    nxt = 1 - cur
                    Ac, Atc, An, Atn = n.A[cur], n.At[cur], n.A[nxt], n.At[nxt]
                    pa, pb, pc = B1, B2, B3
                    for j in range(8):
                        js = slice(j * 64, (j + 1) * 64)
                        mm(pb[0:64, js], Ac[:, js], Atc[:, js], True, True, [Ac.d, Atc.d], [pb.d], inc=(j == 7))
                    cp(Atn[:], pb[0:64, :], [pb.d], [Atn.d], eng="act")
                    if l < 5:
                        for j in range(8):
                            js = slice(j * 64, (j + 1) * 64)
                            mm(pa[0:64, js], Atc[:, js], Ac[:, js], True, True, [Ac.d, Atc.d], [pa.d], inc=(j == 7))
                        cp(An[:], pa[0:64, :], [pa.d], [An.d], eng="dve")
                    if YS(10):
                        yield
                    for j in range(8):
                        js = slice(j * 64, (j + 1) * 64)
                        mm(pc[0:64, js], Atn[:, js], X[:, js], True, True, [Atn.d, X.d], [pc.d], inc=(j == 7))
                    tt(X[:], pc[0:64, :], X[:].bitcast(F32), ALU.add, [pc.d, X.d], [X.d])
                    cur = nxt
                    if YS(11):
                        yield
                vt, kt_ = n.vtok, n.ktok
                tt(n.vb[:], vt[:], bcc.unsqueeze(2).broadcast_to([64, 8, 128]), ALU.mult, [vt.d, bcol.d], [n.vb.d], eng="pool")
                tt(n.tmpk[:], kt_[:], bcc.unsqueeze(2).broadcast_to([64, 8, 128]), ALU.mult, [kt_.d, bcol.d], [n.tmpk.d], eng="pool")
                tt(n.kbg[:], n.tmpk[:], egc.unsqueeze(2).broadcast_to([64, 8, 128]), ALU.mult, [n.tmpk.d, egccol.d], [n.kbg.d], eng="pool")
                if YS(12):
                    yield
                tt(n.kds[:], gcb[0:64, 63::64], gcc, ALU.subtract, [gcb.d, gccol.d], [n.kds.d])
                act(n.kds[:], n.kds[:], AF.Exp, [n.kds.d], [n.kds.d])
                tt(n.kdec[:], kt_[:], n.kds[:].unsqueeze(2).broadcast_to([64, 8, 128]), ALU.mult, [kt_.d, n.kds.d], [n.kdec.d], eng="pool")
                if YS(13):
                    yield
                for half in range(2):
                    p = B0 if half == 0 else B1
                    for j4 in range(4):
                        j = half * 4 + j4
                        mm(p[0:64, j4 * 128:(j4 + 1) * 128], X[:, j * 64:(j + 1) * 64], n.vb[:, j, :], True, True, [X.d, n.vb.d], [p.d],
                           inc=(j4 == 3))
                    cp(n.u[:, half * 4:(half + 1) * 4, :].rearrange("p a b -> p (a b)"), p[0:64, :], [p.d], [n.u.d],
                       eng=("act" if half else "dve"))
                for j in range(8):
                    mm(B2[:, j * 64:(j + 1) * 64], n.kbg[:, j, :], X[:, j * 64:(j + 1) * 64], True, True, [n.kbg.d, X.d], [B2.d], inc=(j == 7))
                cp(n.wT[:], B2[:], [B2.d], [n.wT.d], eng="act")
                tt(n.qdT[:], qT[:].bitcast(F32), egcb[:], ALU.mult, [qT.d, egcb.d], [n.qdT.d])
                if YS(14):
                    yield
                for j in range(8):
                    js = slice(j * 64, (j + 1) * 64)
                    p1, p2, p3 = B3, B0, B1
                    mm(p1[0:64, 0:128], n.wT[:, js], St[:], True, True, [n.wT.d, St.d], [p1.d])
                    if YS(15):
                        yield
                    tt(n.vnew[:], n.u[:, j, :], p1[0:64, 0:128], ALU.subtract, [n.u.d, p1.d], [n.vnew.d])
                    if YS(16):
                        yield
                    mm(p2[0:64, 0:128], n.qdT[:, js], St[:], True, False, [n.qdT.d, St.d], [p2.d], inc=False)
                    mm(p2[0:64, 0:128], n.attnT[:, js], n.vnew[:], False, True, [n.attnT.d, n.vnew.d], [p2.d])
                    mm(p3[:, 0:128], n.kdec[:, j, :], n.vnew[:], True, True, [n.kdec.d, n.vnew.d], [p3.d])
                    if YS(17):
                        yield
                    stt(St[:], St[:].bitcast(F32), egcb[:, j * 64 + 63:j * 64 + 64], p3[:, 0:128], ALU.mult, ALU.add,
                        [St.d, egcb.d, p3.d], [St.d])
                    cp(n.ob[:, j, :], p2[0:64, 0:128], [p2.d], [n.ob.d], eng="act")
                    if YS(18):
                        yield
                ob, oss = n.ob, n.oss
                act(n.tmpk[:], ob[:], AF.Square, [ob.d], [n.tmpk.d])
                fw.op("dve", lambda e: e.tensor_reduce(out=oss[:], in_=n.tmpk[:], axis=AX.X, op=ALU.add), [n.tmpk.d], [oss.d])
                act(oss[:], oss[:], AF.Sqrt, [oss.d, epsc.d], [oss.d], scale=1.0 / 128.0, bias=epsc[0:64, :])
                recip(oss[:], oss[:], [oss.d], [oss.d])
                if YS(19):
                    yield
                tt(ob[:], ob[:], oss[:].unsqueeze(2).broadcast_to([64, 8, 128]), ALU.mult, [ob.d, oss.d], [ob.d])
                tt(ob[:], ob[:], onwt[:].unsqueeze(1).broadcast_to([64, 8, 128]), ALU.mult, [ob.d, onwt.d], [ob.d])
                for j in range(8):
                    tr(B2[:, j * 64:(j + 1) * 64], ob[:, j, :], ident[0:64, 0:64], [ob.d, ident.d], [B2.d], inc=(j == 7))
                fw.dma(n.zr[:], PT[3072 + h * 128:3072 + (h + 1) * 128, bs], reads=[PTd], writes=[n.zr.d])
                act(n.zr[:], n.zr[:], AF.Silu, [n.zr.d], [n.zr.d])
                if YS(20):
                    yield
                tt(n.oo[:], B2[:], n.zr[:], ALU.mult, [B2.d, n.zr.d], [n.oo.d])
                fw.dma(OT[h * 128:(h + 1) * 128, bs], n.oo[:], reads=[n.oo.d], writes=[Dep()])
                if YS(21):
                    yield

            def gchain(g, delay):
                for _ in range(delay):
                    yield
                for b in range(4):
                    for hp in range(4):
                        for _ in gdn(2 * hp + g, b, slots[g]):
                            yield

            run_rr([gchain(0, 0), gchain(1, GSTAG)], "gdn")
            fw.barrier()

    def phase_mix1():
        with ExitStack() as ph:
            trif = sb("trif", [128, 128], F32, ph)
            tri = sb("tri", [128, 128], BF16, ph)
            selh = sb("selh", [8, 8, 128], F32, ph)
            selpf = sb("selpf", [128, 8, 128], F32, ph)
            selr = sb("selr", [128, 8, 128], BF16, ph)
            gv = sb("gv", [8, 3], F32, ph)
            rope = sb("rope", [32, 2, S], F32, ph)
            mtab = sb("mtab", [128, 3, 16, 8], F32, ph)
            Fcol = sb("Fcol", [128, 16, 8], F32, ph)
            sel3f = sb("sel3f", [128, 8, 128], F32, ph)
            sel3 = sb("sel3", [128, 8, 128], BF16, ph)
            F3 = sb("F3", [128, S], BF16, ph)
            fw.dma(sel3f[:], c_sel3, writes=[sel3f.d])
            cp(sel3[:], sel3f[:], [sel3f.d], [sel3.d], eng="act")
            fw.dma(trif[:], c_tri, writes=[trif.d])
            fw.dma(selh[:], c_selh, writes=[selh.d])
            fw.dma(selpf[:], c_selp, writes=[selpf.d])
            fw.dma(gv[:], gdnvec, writes=[gv.d])
            fw.dma(rope[:], c_rope, writes=[rope.d])
            fw.dma(mtab[:], c_moba, writes=[mtab.d])
            cp(selr[:], selpf[:], [selpf.d], [selr.d], eng="act")
            ts(tri[:], trif[:], -1.0, -NEG / SCALE, ALU.add, ALU.mult, [trif.d], [tri.d])
            identb = sb("identb", [128, 128], BF16, ph)
            cp(identb[:], ident[:], [ident.d], [identb.d], eng="act")
            with ExitStack() as pre:
                onesrow = sb("onesrow", [8, S], F32, pre)
                ffr = sb("ffr", [8, S], F32, pre)
                Fr = sb("Fr", [8, S], F32, pre)
                negfb = sb("negfb", [8, 1], F32, pre)
                fw.op("dve", lambda e: e.memset(onesrow[:], 1.0), [], [onesrow.d])
                fw.dma(ffr[:], PT[6144:6152, :], reads=[PTd], writes=[ffr.d])
                ts(negfb[:], gv[:, 2:3], -1.0, None, ALU.mult, None, [gv.d], [negfb.d])
                act(ffr[:], ffr[:], AF.Exp, [ffr.d, negfb.d], [ffr.d], scale=-1.0, bias=negfb[:, 0:1])
                act(ffr[:], ffr[:], AF.Ln, [ffr.d, onec.d], [ffr.d], bias=onec[0:8, :])
                ts(ffr[:], ffr[:], -1.0, None, ALU.mult, None, [ffr.d], [ffr.d])
                fw.op("dve", lambda e: e.tensor_tensor_scan(out=Fr[:], data0=onesrow[:], data1=ffr[:], initial=0.0, op0=ALU.mult,
                                                            op1=ALU.add), [onesrow.d, ffr.d], [Fr.d])
                p = ps[0]
                for kt in range(16):
                    tr(p[:, kt * 8:(kt + 1) * 8], Fr[:, kt * 128:(kt + 1) * 128], ident[0:8, 0:8], [Fr.d, ident.d], [p.d], inc=(kt == 15))
                cp(Fcol[:].rearrange("p a b -> p (a b)"), p[:, 0:128], [p.d], [Fcol.d])
                ts(Fcol[:], Fcol[:], -1.0, None, ALU.mult, None, [Fcol.d], [Fcol.d])
                F3f = sb("F3f", [128, S], F32, pre)
                hb = sb("hb", [8, S], BF16, pre)
                hf = sb("hf", [8, S], F32, pre)
                mf = sb("mf", [8, S], F32, pre)
                fw.op("dve", lambda e: e.memset(F3f[:], 0.0), [], [F3f.d])
                ts(Fr[:], Fr[:], 1.0 / SCALE, None, ALU.mult, None, [Fr.d], [Fr.d])
                cp(hb[:], Fr[:], [Fr.d], [hb.d])
                cp(hf[:], hb[:], [hb.d], [hf.d])
                tt(Fr[:], Fr[:], hf[:], ALU.subtract, [Fr.d, hf.d], [Fr.d])
                cp(hb[:], Fr[:], [Fr.d], [hb.d])
                cp(mf[:], hb[:], [hb.d], [mf.d])
                tt(Fr[:], Fr[:], mf[:], ALU.subtract, [Fr.d, mf.d], [Fr.d])
                fw.dma(F3f[0:8, :], hf[:], reads=[hf.d], writes=[F3f.d])
                fw.dma(F3f[32:40, :], mf[:], reads=[mf.d], writes=[F3f.d])
                fw.dma(F3f[64:72, :], Fr[:], reads=[Fr.d], writes=[F3f.d])
                cp(F3[:], F3f[:], [F3f.d], [F3.d], eng="act")
                fw.barrier()

            def mks(g):
                n = NS()
                n.B = [ps[4 * g + i] for i in range(4)]
                n.xq = sb("xq", [128, S], F32, ph)
                n.xk = sb("xk", [128, S], F32, ph)
                n.xv = sb("xv", [128, S], F32, ph)
                n.xsw = sb("xsw", [32, S], F32, ph)
                n.qT = sb("qT", [128, S], BF16, ph)
                n.kT = sb("kT", [128, S], BF16, ph)
                n.vtok = sb("vtok", [128, 16, 128], BF16, ph)
                n.pt = [sb("pt%d" % i, [128, 512], BF16, ph) for i in range(4)]
                n.rec = sb("rec", [128, 512], F32, ph)
                n.oo = [sb("oo%d" % i, [128, 512], BF16, ph) for i in range(2)]
                n.kmean = sb("kmean", [128, 8], F32, ph)
                n.gm = sb("gm", [128, 16, 8], F32, ph)
                n.top8 = sb("top8", [128, 16, 8], F32, ph)
                n.nm = sb("nm", [128, 16, 8], F32, ph)
                n.nmT = sb("nmT", [128, S], BF16, ph)
                fw.op("dve", lambda e: e.memset(n.nmT[:], 0.0), [], [n.nmT.d])
                return n

            aslots = [mks(0), mks(1)]

            def attn(hh, n):
                moba = hh >= 8
                h = hh % 8
                base = 3072 if moba else 0
                B0, B1, B2, B3 = n.B
                xq, xk, xv, xsw, qT, kT, vtok = n.xq, n.xk, n.xv, n.xsw, n.qT, n.kT, n.vtok
                fw.dma(xq[:], PT[base + h * 128:base + (h + 1) * 128, :], reads=[PTd], writes=[xq.d])
                fw.dma(xk[:], PT[base + 1024 + h * 128:base + 1024 + (h + 1) * 128, :], reads=[PTd], writes=[xk.d])
                fw.dma(xv[:], PT[base + 2048 + h * 128:base + 2048 + (h + 1) * 128, :], reads=[PTd], writes=[xv.d])
                if moba:
                    for row0, dst in ((base + h * 128, xq), (base + 1024 + h * 128, xk)):
                        fw.dma(xsw[0:16, :], PT[row0 + 16:row0 + 32, :], reads=[PTd], writes=[xsw.d])
                        fw.dma(xsw[16:32, :], PT[row0:row0 + 16, :], reads=[PTd], writes=[xsw.d])
                        tt(dst[0:32, :], dst[0:32, :], rope[:, 0, :], ALU.mult, [dst.d, rope.d], [dst.d])
                        tt(xsw[:], xsw[:], rope[:, 1, :], ALU.mult, [xsw.d, rope.d], [xsw.d])
                        tt(dst[0:32, :], dst[0:32, :], xsw[:], ALU.add, [dst.d, xsw.d], [dst.d])
                        yield
                cp(qT[:], xq[:], [xq.d], [qT.d], eng="act")
                cp(kT[:], xk[:], [xk.d], [kT.d], eng="dve")
                yield
                for g4 in range(4):
                    p = B0 if g4 % 2 == 0 else B1
                    for j in range(4):
                        kt = g4 * 4 + j
                        tr(p[:, j * 128:(j + 1) * 128], xv[:, kt * 128:(kt + 1) * 128], ident[:], [xv.d, ident.d], [p.d], inc=(j == 3))
                    cp(vtok[:, g4 * 4:(g4 + 1) * 4, :].rearrange("p a b -> p (a b)"), p[:], [p.d], [vtok.d],
                       eng=("act" if g4 % 2 else "dve"))
                    yield
                if moba:
                    kmean, gm, top8, nm, nmT = n.kmean, n.gm, n.top8, n.nm, n.nmT
                    fw.op("dve", lambda e: e.tensor_reduce(out=kmean[:], in_=xk[:].rearrange("p (a b) -> p a b", b=256), axis=AX.X,
                                                           op=ALU.add), [xk.d], [kmean.d])
                    ts(kmean[:], kmean[:], 1.0 / 256.0, None, ALU.mult, None, [kmean.d], [kmean.d])
                    for qt in range(16):
                        mm(B2[:, qt * 8:(qt + 1) * 8], xq[:, qt * 128:(qt + 1) * 128], kmean[:], True, True, [xq.d, kmean.d], [B2.d], inc=(qt == 15))
                    yield
                    tt(gm[:].rearrange("p a b -> p (a b)"), B2[:, 0:128], mtab[:, 0, :, :].rearrange("p a b -> p (a b)"), ALU.add,
                       [B2.d, mtab.d], [gm.d])
                    for qt in range(16):
                        fw.op("dve", lambda e: e.max(out=top8[:, qt, :], in_=gm[:, qt, :]), [gm.d], [top8.d])
                    yield
                    tt(nm[:], gm[:], top8[:, :, 2:3].broadcast_to([128, 16, 8]), ALU.is_ge, [gm.d, top8.d], [nm.d])
                    tt(nm[:], nm[:], mtab[:, 1, :, :], ALU.mult, [nm.d, mtab.d], [nm.d])
                    tt(nm[:], nm[:], mtab[:, 2, :, :], ALU.add, [nm.d, mtab.d], [nm.d])
                    ts(nm[:], nm[:], -NEG / SCALE, NEG / SCALE, ALU.mult, ALU.add, [nm.d], [nm.d])
                    yield
                    for g4 in range(4):
                        for j in range(4):
                            qt = g4 * 4 + j
                            tr(B3[0:8, j * 128:(j + 1) * 128], nm[:, qt, :], ident[:], [nm.d, ident.d], [B3.d], inc=(j == 3))
                        cp(nmT[0:8, g4 * 512:(g4 + 1) * 512], B3[0:8, :], [B3.d], [nmT.d], eng="act")
                        yield
                pO, pS = B2, B3
                steps = [(qi, kt) for qi in range(4) for kt in range(4 * qi + 4)]

                def geo(i):
                    qi, kt = steps[i]
                    r_ = kt - 4 * qi
                    return qi, kt, r_, max(r_, 0) * 128, qi * 512

                def qk(i):
                    qi, kt, r_, c0, q0 = geo(i)
                    pq = B0 if i % 2 == 0 else B1
                    mm(pq[:, c0:512], kT[:, kt * 128:(kt + 1) * 128], qT[:, q0 + c0:q0 + 512], True, False, [kT.d, qT.d], [pq.d],
                       inc=False)
                    diag = r_ >= 0
                    if moba:
                        mm(pq[:, c0:512], selr[:, kt // 2, :], n.nmT[:, q0 + c0:q0 + 512], False, not diag, [selr.d, n.nmT.d], [pq.d],
                           inc=(not diag))
                    else:
                        mm(pq[:, c0:512], sel3[:, h, :], F3[:, q0 + c0:q0 + 512], False, not diag, [sel3.d, F3.d], [pq.d], inc=(not diag))
                    if diag:
                        mm(pq[:, c0:c0 + 128], identb[:], tri[:], False, True, [identb.d, tri.d], [pq.d])

                qk(0)
                yield
                for i in range(len(steps)):
                    qi, kt, r_, c0, q0 = geo(i)
                    nkt = 4 * qi + 4
                    pq = B0 if i % 2 == 0 else B1
                    P = n.pt[i % 4]
                    if i + 1 < len(steps):
                        qk(i + 1)
                        yield
                    if moba:
                        act(P[:, c0:512], pq[:, c0:512], AF.Exp, [pq.d], [P.d], scale=SCALE)
                    else:
                        act(P[:, c0:512], pq[:, c0:512], AF.Exp, [pq.d, Fcol.d], [P.d], scale=SCALE, bias=Fcol[:, kt, h:h + 1])
                    yield
                    mm(pO[:, c0:512], vtok[:, kt, :], P[:, c0:512], kt == 0, kt == nkt - 1, [vtok.d, P.d], [pO.d], inc=False)
                    mm(pS[:, c0:512], ones_bf[:], P[:, c0:512], kt == 0, kt == nkt - 1, [ones_bf.d, P.d], [pS.d])
                    if kt == nkt - 1:
                        recip(n.rec[:], pS[:], [pS.d], [n.rec.d])
                        o = n.oo[qi % 2]
                        tt(o[:], pO[:], n.rec[:], ALU.mult, [pO.d, n.rec.d], [o.d])
                        fw.dma(OT[hh * 128:(hh + 1) * 128, qi * 512:(qi + 1) * 512], o[:], reads=[o.d], writes=[Dep()])
                    yield

            def achain(g, delay):
                for _ in range(delay):
                    yield
                for hp in range(8):
                    for _ in attn(2 * hp + g, aslots[g]):
                        yield

            run_rr([achain(0, 0), achain(1, ASTAG)], "attn")
            fw.barrier()


    fw.barrier()
    stages = [
        ("in0", lambda: phase_inproj(w_in0, F0, 0, src=xT, order1=([c_ for n_ in range(8) for c_ in (32 + n_, 40 + n_)] + [48] + list(range(0, 32))), side=(None if lim is not None else lru_side))),
        ("mix0", phase_mix0),
        ("ffn0", lambda: phase_ffn(0, 1, Wout=w_out0, res_src=xT)),
        ("in1", lambda: phase_inproj(w_in1, F1, 2)),
        ("mix1", phase_mix1),
        ("ffn1", lambda: phase_ffn(1, 3, Wout=w_out1, res_src=HT, fin=True)),
    ]
    for name, fn in stages:
        fn()
        if stop_after == name:
            break
    fw.barrier()
    es.close()
    return nc


def _consts():
    c = {}
    c["c_ident"] = np.eye(128, dtype=np.float32)
    s = np.arange(64)[:, None]
    cc = np.arange(64)[None, :]
    mI = (s <= cc).astype(np.float32)
    mS = -(s < cc).astype(np.float32)
    I64 = np.eye(64, dtype=np.float32)
    c["c_mask64"] = np.ascontiguousarray(np.stack([np.tile(mI, (1, 8)), np.tile(mS, (1, 8)), np.tile(I64, (1, 8))], axis=1))
    k = np.arange(128)[:, None]
    q = np.arange(128)[None, :]
    c["c_tri"] = (k <= q).astype(np.float32)
    r = np.ones((8, S), np.float32)
    r[:, ::64] = 0.0
    c["c_reset"] = r
    sel = np.zeros((8, 8, 128), np.float32)
    for h in range(8):
        sel[h, h, :] = 1.0
    c["c_selh"] = sel
    selp = np.zeros((128, 8, 128), np.float32)
    selp[0:8] = sel
    c["c_selp"] = selp
    sel3 = np.zeros((128, 8, 128), np.float32)
    for h in range(8):
        sel3[h, h, :] = 1.0
        sel3[32 + h, h, :] = 1.0
        sel3[64 + h, h, :] = 1.0
    c["c_sel3"] = sel3
    half = 16
    inv = (500000.0 ** (-np.arange(half, dtype=np.float32) / half)).astype(np.float32)
    ang = (np.arange(S, dtype=np.float32)[None, :] * inv[:, None]).astype(np.float32)
    cos = np.cos(ang.astype(np.float64)).astype(np.float32)
    sin = np.sin(ang.astype(np.float64)).astype(np.float32)
    rope = np.zeros((32, 2, S), np.float32)
    rope[0:16, 0] = cos
    rope[16:32, 0] = cos
    rope[0:16, 1] = -sin
    rope[16:32, 1] = sin
    c["c_rope"] = rope
    mt = np.zeros((128, 3, 16, 8), np.float32)
    for qt in range(16):
        b = qt // 2
        for j in range(8):
            mt[:, 0, qt, j] = 0.0 if j < b else -1e30
            mt[:, 1, qt, j] = 1.0 if j < b else 0.0
            mt[:, 2, qt, j] = 1.0 if j == b else 0.0
    c["c_moba"] = mt
    return c


def _prep_weights(inp):
    f = lambda a: np.ascontiguousarray(np.asarray(a, dtype=np.float32))
    w = {}
    wi0 = inp["ab_w_in"][0]
    w["w_in0"] = f(np.concatenate([wi0[:, 0:4096], wi0[:, 4112:6160], wi0[:, 4096:4112]], axis=1))
    wi1 = inp["cd_w_in"][0]
    w["w_in1"] = f(np.concatenate([wi1[:, 0:3072], wi1[:, 3080:6152], wi1[:, 3072:3080]], axis=1))
    w["w_out0"] = f(inp["ab_w_out"][0])
    w["w_out1"] = f(inp["cd_w_out"][0])
    w["w_gate"] = f(inp["ffn_w_gate"])
    w["w_up"] = f(inp["ffn_w_up"])
    w["w_down"] = f(inp["ffn_w_down"])
    nw = np.stack([inp["ab_norm"][0], inp["ffn_norm"][0], inp["cd_norm"][0], inp["ffn_norm"][1], inp["final_norm"]], axis=0)
    w["normw"] = f(nw.reshape(5, NCH, 128).transpose(2, 0, 1))
    w["convqkv"] = f(inp["ab_conv_qkv"][0].T.reshape(24, 128, 4).transpose(1, 0, 2))
    w["lruconv"] = f(inp["ab_lru_conv_w"][0].T.reshape(8, 128, 4).transpose(1, 0, 2))
    lv = np.stack([inp["ab_lru_conv_b"][0], inp["ab_lru_ba"][0], inp["ab_lru_bx"][0], inp["ab_lru_lambda"][0]], axis=0)
    w["lruvec"] = f(lv.reshape(4, 8, 128).transpose(2, 0, 1))
    w["lruwa"] = f(np.asarray(inp["ab_lru_wa"][0]).transpose(1, 0, 2))
    w["lruwx"] = f(np.asarray(inp["ab_lru_wx"][0]).transpose(1, 0, 2))
    w["gdnvec"] = f(np.stack([inp["ab_a_log"][0], inp["ab_dt_bias"][0], inp["cd_f_bias"][0]], axis=1))
    w["onw"] = f(np.broadcast_to(np.asarray(inp["ab_out_norm"][0])[None, :], (64, 128)))
    w.update(_consts())
    return w


_NC_CACHE = {}


def kernel(**inputs):
    inp = {k: np.asarray(v) for k, v in inputs.items()}
    x = inp["x"]
    B = x.shape[0]
    w = _prep_weights(inp)
    if "nc" not in _NC_CACHE:
        _NC_CACHE["nc"] = build()
    nc = _NC_CACHE["nc"]
    in_maps = []
    for b in range(B):
        m = dict(w)
        m["xT"] = np.ascontiguousarray(x[b].T.astype(np.float32))
        in_maps.append(m)
    res = run_bass_kernel_spmd(nc, in_maps, core_ids=list(range(B)))
    out = np.stack([np.ascontiguousarray(r["outT"].T) for r in res.results], axis=0)
    return out.astype(np.float32)
```

```python
import os
import numpy as np
from contextlib import ExitStack
_os_env = os.environ
import concourse.bass as bass
import concourse.mybir as mybir
from concourse.bass_utils import run_bass_kernel_spmd

F32 = mybir.dt.float32
F32R = mybir.dt.float32r
BF16 = mybir.dt.bfloat16
ALU = mybir.AluOpType
AF = mybir.ActivationFunctionType
AX = mybir.AxisListType

S = 2048
D = 2048
DFF = 5632
NCH = 16
F0 = 6160
F1 = 6152
SCALE = 128.0 ** -0.5
GSTAG = int(_os_env.get("GSTAG", "0"))
ASTAG = int(_os_env.get("ASTAG", "0"))
NEG = -30000.0


import os as _os0
FORCE_INC = bool(_os0.environ.get("KINC"))


class Dep:
    __slots__ = ("w", "r")

    def __init__(self):
        self.w = None
        self.r = []


class FW:
    def __init__(self, nc, es, n_dma_sems=40):
        self.nc = nc
        self.eng = {"pe": nc.tensor, "dve": nc.vector, "act": nc.scalar, "pool": nc.gpsimd, "sp": nc.sync}
        self.sem = {k: es.enter_context(nc.semaphore("s_" + k)) for k in self.eng}
        self.cnt = {k: 0 for k in self.eng}
        self.waited = {k: {} for k in self.eng}
        self.dsem = [es.enter_context(nc.semaphore("d%d" % i)) for i in range(n_dma_sems)]
        self.dcnt = [0] * n_dma_sems
        self.dnext = 0

    def _wait(self, e, tok):
        if tok is None:
            return
        sem, val = tok
        if e == "pe" and sem is self.sem["pe"]:
            return
        key = id(sem)
        if self.waited[e].get(key, 0) >= val:
            return
        self.eng[e].wait_ge(sem, val)
        self.waited[e][key] = val

    def _deps(self, e, reads, writes):
        for b in reads:
            self._wait(e, b.w)
        for b in writes:
            self._wait(e, b.w)
            for t in b.r:
                self._wait(e, t)

    def _commit(self, tok, reads, writes):
        for b in writes:
            b.w = tok
            b.r = []
        for b in reads:
            if not any(b is w for w in writes):
                b.r.append(tok)
                if len(b.r) > 24:
                    b.r = b.r[-24:] if False else b.r

    def op(self, e, fn, reads=(), writes=(), inc=True):
        if FORCE_INC:
            inc = True
        self._deps(e, reads, writes)
        ins = fn(self.eng[e])
        if inc:
            self.cnt[e] += 1
            ins.then_inc(self.sem[e], 1)
            tok = (self.sem[e], self.cnt[e])
        else:
            tok = (self.sem[e], self.cnt[e] + 1)
        self._commit(tok, reads, writes)
        return tok

    def dma(self, out, in_, reads=(), writes=(), q="sp"):
        i = self.dnext
        self.dnext = (self.dnext + 1) % len(self.dsem)
        if self.dcnt[i] > 0:
            self._wait(q, (self.dsem[i], 16 * self.dcnt[i]))
        self._deps(q, reads, writes)
        self.eng[q].dma_start(out=out, in_=in_).then_inc(self.dsem[i], 16)
        self.dcnt[i] += 1
        tok = (self.dsem[i], 16 * self.dcnt[i])
        self._commit(tok, reads, writes)
        return tok

    def barrier(self):
        for e in self.eng:
            for o in self.eng:
                if o != e and self.cnt[o] > 0:
                    self._wait(e, (self.sem[o], self.cnt[o]))
            for i, s in enumerate(self.dsem):
                if self.dcnt[i] > 0:
                    self._wait(e, (s, 16 * self.dcnt[i]))


class T:
    def __init__(self, h):
        self.t = h
        self.d = Dep()

    def __getitem__(self, k):
        return self.t[k]


def build(debug=False, stop_after=None, lim=None):
    nc = bass.Bass("TRN2", target_bir_lowering=False)
    es = ExitStack()
    fw = FW(nc, es)

    def din(name, shape, dt=F32):
        return nc.dram_tensor(name, list(shape), dt, kind="ExternalInput").ap()

    def dscr(name, shape, dt=F32):
        return nc.dram_tensor(name, list(shape), dt, kind=("ExternalOutput" if debug else "Internal")).ap()

    xT = din("xT", [D, S])
    w_in0 = din("w_in0", [D, F0])
    w_in1 = din("w_in1", [D, F1])
    w_out0 = din("w_out0", [D, D])
    w_out1 = din("w_out1", [D, D])
    w_gate = din("w_gate", [2, D, DFF])
    w_up = din("w_up", [2, D, DFF])
    w_down = din("w_down", [2, DFF, D])
    normw = din("normw", [128, 5, NCH])
    convqkv = din("convqkv", [128, 24, 4])
    lruconv = din("lruconv", [128, 8, 4])
    lruvec = din("lruvec", [128, 4, 8])
    lruwa = din("lruwa", [128, 8, 128])
    lruwx = din("lruwx", [128, 8, 128])
    gdnvec = din("gdnvec", [8, 3])
    onw = din("onw", [64, 128])
    c_ident = din("c_ident", [128, 128])
    c_mask64 = din("c_mask64", [64, 3, 512])
    c_tri = din("c_tri", [128, 128])
    c_reset = din("c_reset", [8, S])
    c_selh = din("c_selh", [8, 8, 128])
    c_selp = din("c_selp", [128, 8, 128])
    c_sel3 = din("c_sel3", [128, 8, 128])
    c_rope = din("c_rope", [32, 2, S])
    c_moba = din("c_moba", [128, 3, 16, 8])
    outT = nc.dram_tensor("outT", [D, S], F32, kind="ExternalOutput").ap()
    HT = dscr("HT", [D, S])
    PT = dscr("PT", [F0, S])
    OT = dscr("OT", [D, S], BF16)

    uid = [0]

    def sb(name, shape, dt=F32, stack=None):
        uid[0] += 1
        return T((stack or es).enter_context(nc.sbuf_tensor("%s_%d" % (name, uid[0]), list(shape), dt)))

    ps = [T(es.enter_context(nc.psum_tensor("ps%d" % i, [128, 512], F32))) for i in range(8)]

    def mm(out, lhsT, rhs, start, stop, R, W, inc=True):
        return fw.op("pe", lambda e: e.matmul(out, lhsT, rhs, start=start, stop=stop), R, W, inc=inc)

    def tr(out, in_, ident, R, W, inc=True):
        return fw.op("pe", lambda e: e.transpose(out, in_, ident), R, W, inc=inc)

    def act(out, in_, func, R, W, **kw):
        return fw.op("act", lambda e: e.activation(out=out, in_=in_, func=func, **kw), R, W)

    def tt(out, in0, in1, op, R, W, eng="dve"):
        return fw.op(eng, lambda e: e.tensor_tensor(out=out, in0=in0, in1=in1, op=op), R, W)

    def ts(out, in0, s1, s2, op0, op1, R, W, eng="dve"):
        if s2 is None:
            return fw.op(eng, lambda e: e.tensor_scalar(out=out, in0=in0, scalar1=s1, scalar2=None, op0=op0), R, W)
        return fw.op(eng, lambda e: e.tensor_scalar(out=out, in0=in0, scalar1=s1, scalar2=s2, op0=op0, op1=op1), R, W)

    def stt(out, in0, scalar, in1, op0, op1, R, W):
        return fw.op("dve", lambda e: e.scalar_tensor_tensor(out=out, in0=in0, scalar=scalar, in1=in1, op0=op0, op1=op1), R, W)

    def cp(out, in_, R, W, eng="dve"):
        if eng == "act":
            return act(out, in_, AF.Copy, R, W)
        return fw.op(eng, lambda e: e.tensor_copy(out=out, in_=in_), R, W)

    def recip(out, in_, R, W):
        return fw.op("dve", lambda e: e.reciprocal(out=out, in_=in_), R, W)

    ident = sb("ident", [128, 128])
    ones_bf = sb("ones_bf", [128, 128], BF16)
    ones_r = sb("ones_r", [128, 128], F32R)
    epsc = sb("epsc", [128, 1])
    onec = sb("onec", [128, 1])
    nwt = sb("nwt", [128, 5, NCH])
    fw.dma(ident[:], c_ident, writes=[ident.d])
    fw.dma(nwt[:], normw, writes=[nwt.d])
    fw.op("dve", lambda e: e.memset(ones_bf[:], 1.0), [], [ones_bf.d])
    fw.op("dve", lambda e: e.memset(epsc[:], 1e-6), [], [epsc.d])
    fw.op("dve", lambda e: e.memset(onec[:], 1.0), [], [onec.d])
    ts(ones_r[:], ident[:], 0.0, 1.0, ALU.mult, ALU.add, [ident.d], [ones_r.d])

    HTd = [[Dep() for _ in range(2)] for _ in range(NCH)]
    PTd = Dep()
    OTd = Dep()
    PTf = {}

    evac_flip = [0]

    def run_rr(gens, tag=""):
        gens = list(gens)
        import os as _os
        if tag and tag in _os.environ.get("KSEQ", "").split(","):
            for g in gens:
                for _ in g:
                    pass
            return
        while gens:
            for g in list(gens):
                try:
                    next(g)
                except StopIteration:
                    gens.remove(g)

    class NS:
        pass

    def fast(gen, n):
        while True:
            for _ in range(n):
                try:
                    next(gen)
                except StopIteration:
                    return
            yield

    import os as _os1
    _ys = {1, 2} | set(range(7, 22))

    def YS(k):
        return k in _ys

    def norm_gen(th, nidx, hn, hbuf, sqb, rstd, src=HT, hbuf2=None):
        PF = 4
        for t2 in range(2):
            hb = hbuf if (t2 == 0 or hbuf2 is None) else hbuf2
            hd = hb.hd
            c0 = th * 1024 + t2 * 512
            pn = ps[6 + t2]

            def load(c):
                fw.dma(hb[:, c, :], src[c * 128:(c + 1) * 128, c0:c0 + 512], reads=[HTd[c][th]], writes=[hd[c]])

            for c in range(PF):
                load(c)
            for c in range(NCH):
                if c + PF < NCH:
                    load(c + PF)
                sq = sqb[c % 2]
                act(sq[:], hb[:, c, :], AF.Square, [hd[c]], [sq.d])
                mm(pn[:], ones_bf[:], sq[:], c == 0, c == NCH - 1, [ones_bf.d, sq.d], [pn.d])
                yield
            act(rstd[:], pn[:], AF.Ln, [pn.d, epsc.d], [rstd.d], scale=1.0 / D, bias=epsc[:])
            act(rstd[:], rstd[:], AF.Exp, [rstd.d], [rstd.d], scale=-0.5)
            for c in range(NCH):
                stt(hn[:, c, t2 * 512:(t2 + 1) * 512], hb[:, c, :], nwt[:, nidx, c:c + 1], rstd[:], ALU.mult, ALU.mult,
                    [hd[c], nwt.d, rstd.d], [hn.d])
                yield

    def norm_half(*a, **k):
        for _ in norm_gen(*a, **k):
            pass

    def proj_gen(W, Fdim, nk, actT, wring, consume, pre=None, order=None):
        Wv = W.rearrange("(k p) f -> p k f", p=128)
        nfc = (Fdim + 127) // 128
        if lim is not None:
            nfc = min(nfc, lim)
        seq = list(range(nfc)) if order is None else [f_ for f_ in order if f_ < nfc]
        for si, fc in enumerate(seq):
            m = min(128, Fdim - fc * 128)
            wb = wring[si % len(wring)]
            fw.dma(wb[:, 0:nk, 0:m], Wv[:, :, fc * 128:fc * 128 + m], writes=[wb.d], q="pool")
            for t2 in range(2):
                p = ps[(si % 3) * 2 + t2]
                if pre is not None:
                    pre(fc, m, t2)
                for k in range(nk):
                    mm(p[0:m, :], wb[:, k, 0:m], actT[:, k, t2 * 512:(t2 + 1) * 512], k == 0, k == nk - 1,
                       [wb.d, (actT.kd[k] if hasattr(actT, "kd") else actT.d)], [p.d], inc=(k == nk - 1))
                consume(fc, m, t2, p)
                yield

    def proj(*a, **k):
        for _ in proj_gen(*a, **k):
            pass

    def phase_inproj(Win, Fdim, nidx, src=HT, order1=None, side=None):
        with ExitStack() as ph:
            hns = [sb("hn%d" % i, [128, NCH, 1024], BF16, ph) for i in range(2)]
            hbuf = sb("hbuf", [128, NCH, 512], F32, ph)
            hbuf.hd = [Dep() for _ in range(NCH)]
            if side is None:
                hbuf2 = sb("hbufb", [128, NCH, 512], F32, ph)
                hbuf2.hd = [Dep() for _ in range(NCH)]
            else:
                hbuf2 = None
            sqb = [sb("sq%d" % i, [128, 512], BF16, ph) for i in range(2)]
            rstd = sb("rstd", [128, 512], F32, ph)
            wring = [sb("wr%d" % i, [128, NCH, 128], BF16, ph) for i in range(4)]
            stg = [sb("stg%d" % i, [128, 512], F32, ph) for i in range(4)]

            stgc = [0]

            def mkconsume(th):
                def consume(fc, m, t2, p):
                    st = stg[stgc[0] % 4]
                    stgc[0] += 1
                    evac_flip[0] ^= 1
                    cp(st[0:m, :], p[0:m, :], [p.d], [st.d], eng=("act" if evac_flip[0] else "dve"))
                    dd = Dep()
                    PTf[(fc, th, t2)] = dd
                    fw.dma(PT[fc * 128:fc * 128 + m, th * 1024 + t2 * 512: th * 1024 + (t2 + 1) * 512], st[0:m, :],
                           reads=[st.d], writes=[dd])
                return consume

            norm_half(0, nidx, hns[0], hbuf, sqb, rstd, src=src, hbuf2=hbuf2)
            if lim is None:
                run_rr([proj_gen(Win, Fdim, NCH, hns[0], wring, mkconsume(0)), norm_gen(1, nidx, hns[1], hbuf, sqb, rstd, src=src, hbuf2=hbuf2)])
                if side is None:
                    proj(Win, Fdim, NCH, hns[1], wring, mkconsume(1), order=order1)
                else:
                    run_rr([proj_gen(Win, Fdim, NCH, hns[1], wring, mkconsume(1), order=order1), side(ph)])
            else:
                proj(Win, Fdim, NCH, hns[0], wring, mkconsume(0))
            fw.barrier()

    def phase_outproj(Wout, src=HT):
        with ExitStack() as ph:
            ot = sb("ot", [128, NCH, 1024], BF16, ph)
            wring = [sb("wr%d" % i, [128, NCH, 128], BF16, ph) for i in range(4)]
            hst = [sb("hst%d" % i, [128, 512], F32, ph) for i in range(4)]
            for th in range(2):
                for c in range(NCH):
                    fw.dma(ot[:, c, :], OT[c * 128:(c + 1) * 128, th * 1024:(th + 1) * 1024], reads=[OTd], writes=[ot.d])

                def pre(fc, m, t2, th=th):
                    st = hst[(fc * 2 + t2) % 4]
                    c0 = th * 1024 + t2 * 512
                    fw.dma(st[:], src[fc * 128:(fc + 1) * 128, c0:c0 + 512], reads=[HTd[fc][th]], writes=[st.d])

                def consume(fc, m, t2, p, th=th):
                    st = hst[(fc * 2 + t2) % 4]
                    c0 = th * 1024 + t2 * 512
                    tt(st[:], p[:], st[:], ALU.add, [p.d, st.d], [st.d])
                    fw.dma(HT[fc * 128:(fc + 1) * 128, c0:c0 + 512], st[:], reads=[st.d], writes=[HTd[fc][th]])

                proj(Wout, D, NCH, ot, wring, consume, pre=pre)
            fw.barrier()

    def phase_ffn(layer, nidx, Wout=None, res_src=HT, fin=False):
        with ExitStack() as ph:
            hn = sb("hn", [128, NCH, 1024], BF16, ph)
            hbuf = sb("hbuf", [128, NCH, 512], F32, ph)
            hbuf.hd = [Dep() for _ in range(NCH)]
            sqb = [sb("sq%d" % i, [128, 512], BF16, ph) for i in range(2)]
            rstd = sb("rstd", [128, 512], F32, ph)
            aT = sb("aT", [128, 44, 1024], BF16, ph)
            wg = [sb("wg%d" % i, [128, NCH, 128], BF16, ph) for i in range(2)]
            wu = [sb("wu%d" % i, [128, NCH, 128], BF16, ph) for i in range(2)]
            wd = [sb("wd%d" % i, [128, 44, 128], BF16, ph) for i in range(2)]
            sg = [sb("sg%d" % i, [128, 512], F32, ph) for i in range(2)]
            hst = [sb("hst%d" % i, [128, 512], F32, ph) for i in range(4)]
            Wg = w_gate[layer].rearrange("(k p) f -> p k f", p=128)
            Wu = w_up[layer].rearrange("(k p) f -> p k f", p=128)
            def gateup(th):
                for fc in range(44):
                    g = wg[fc % 2]
                    u = wu[fc % 2]
                    fw.dma(g[:], Wg[:, :, fc * 128:(fc + 1) * 128], writes=[g.d], q="pool")
                    fw.dma(u[:], Wu[:, :, fc * 128:(fc + 1) * 128], writes=[u.d], q="pool")
                    for t2 in range(2):
                        pg = ps[(fc % 2) * 4 + t2 * 2]
                        pu = ps[(fc % 2) * 4 + t2 * 2 + 1]
                        for k in range(NCH):
                            mm(pg[:], g[:, k, :], hn[:, k, t2 * 512:(t2 + 1) * 512], k == 0, k == NCH - 1, [g.d, hn.d], [pg.d],
                               inc=(k == NCH - 1))
                        for k in range(NCH):
                            mm(pu[:], u[:, k, :], hn[:, k, t2 * 512:(t2 + 1) * 512], k == 0, k == NCH - 1, [u.d, hn.d], [pu.d],
                               inc=(k == NCH - 1))
                        s_ = sg[t2]
                        act(s_[:], pg[:], AF.Silu, [pg.d], [s_.d])
                        tt(aT[:, fc, t2 * 512:(t2 + 1) * 512], s_[:], pu[:], ALU.mult, [s_.d, pu.d], [aT.d])

            def mkpre(th):
                def pre(fc, m, t2):
                    st = hst[(fc * 2 + t2) % 4]
                    c0 = th * 1024 + t2 * 512
                    fw.dma(st[:], HT[fc * 128:(fc + 1) * 128, c0:c0 + 512], reads=[HTd[fc][th]], writes=[st.d])
                return pre

            def mkconsume(th):
                def consume(fc, m, t2, p):
                    st = hst[(fc * 2 + t2) % 4]
                    c0 = th * 1024 + t2 * 512
                    tt(st[:], p[:], st[:], ALU.add, [p.d, st.d], [st.d])
                    fw.dma(HT[fc * 128:(fc + 1) * 128, c0:c0 + 512], st[:], reads=[st.d], writes=[HTd[fc][th]])
                return consume

            if Wout is None or lim is not None:
                norm_half(0, nidx, hn, hbuf, sqb, rstd)
            else:
                class V:
                    def __init__(self, ap, d):
                        self.t = ap
                        self.d = d

                    def __getitem__(self, k):
                        return self.t[k]

                ots = []
                for th_ in range(2):
                    v = V(aT[:, th_ * NCH:(th_ + 1) * NCH, :], aT.d)
                    v.kd = [Dep() for _ in range(NCH)]
                    ots.append(v)
                owring = wg + wu

                def oload(th):
                    for c in range(NCH):
                        fw.dma(ots[th][:, c, :], OT[c * 128:(c + 1) * 128, th * 1024:(th + 1) * 1024], reads=[OTd], writes=[ots[th].kd[c]])

                def mkopre(th):
                    def pre(fc, m, t2):
                        st = hst[(fc * 2 + t2) % 4]
                        c0 = th * 1024 + t2 * 512
                        fw.dma(st[:], res_src[fc * 128:(fc + 1) * 128, c0:c0 + 512], reads=[HTd[fc][th]], writes=[st.d])
                    return pre

                def mkocons(th):
                    def consume(fc, m, t2, p):
                        st = hst[(fc * 2 + t2) % 4]
                        c0 = th * 1024 + t2 * 512
                        tt(st[:], p[:], st[:], ALU.add, [p.d, st.d], [st.d])
                        fw.dma(HT[fc * 128:(fc + 1) * 128, c0:c0 + 512], st[:], reads=[st.d], writes=[HTd[fc][th]])
                    return consume

                oload(0)
                oload(1)
                proj(Wout, D, NCH, ots[0], owring, mkocons(0), pre=mkopre(0))
                run_rr([proj_gen(Wout, D, NCH, ots[1], owring, mkocons(1), pre=mkopre(1)), fast(norm_gen(0, nidx, hn, hbuf, sqb, rstd), 2)])
                fw.op("pe", lambda e: e.matmul(ps[0][0:1, 0:2], ones_bf[0:1, 0:1], ones_bf[0:1, 0:2], start=True, stop=True),
                      [ones_bf.d] + ots[0].kd + ots[1].kd + [aT.d], [ps[0].d])
            gateup(0)
            run_rr([proj_gen(w_down[layer], D, 44, aT, wd, mkconsume(0), pre=mkpre(0)), fast(norm_gen(1, nidx, hn, hbuf, sqb, rstd), 2)])
            gateup(1)
            if not fin:
                proj(w_down[layer], D, 44, aT, wd, mkconsume(1), pre=mkpre(1))
            else:
                ost = sg

                def final_gen(th):
                    PF = 4
                    hd = hbuf.hd
                    for t2 in range(2):
                        c0 = th * 1024 + t2 * 512
                        pn = ps[6 + t2]

                        def load(c):
                            fw.dma(hbuf[:, c, :], HT[c * 128:(c + 1) * 128, c0:c0 + 512], reads=[HTd[c][th]], writes=[hd[c]])

                        for c in range(PF):
                            load(c)
                        for c in range(NCH):
                            if c + PF < NCH:
                                load(c + PF)
                            sq = sqb[c % 2]
                            act(sq[:], hbuf[:, c, :], AF.Square, [hd[c]], [sq.d])
                            mm(pn[:], ones_bf[:], sq[:], c == 0, c == NCH - 1, [ones_bf.d, sq.d], [pn.d])
                            yield
                        act(rstd[:], pn[:], AF.Ln, [pn.d, epsc.d], [rstd.d], scale=1.0 / D, bias=epsc[:])
                        act(rstd[:], rstd[:], AF.Exp, [rstd.d], [rstd.d], scale=-0.5)
                        for c in range(NCH):
                            o = ost[c % 2]
                            stt(o[:], hbuf[:, c, :], nwt[:, 4, c:c + 1], rstd[:], ALU.mult, ALU.mult, [hd[c], nwt.d, rstd.d], [o.d])
                            fw.dma(outT[c * 128:(c + 1) * 128, c0:c0 + 512], o[:], reads=[o.d], writes=[Dep()])
                            yield

                run_rr([proj_gen(w_down[layer], D, 44, aT, wd, mkconsume(1), pre=mkpre(1)), fast(final_gen(0), 2)])
                for _ in final_gen(1):
                    pass
            fw.barrier()

    def phase_final():
        with ExitStack() as ph:
            hbuf = sb("hbuf", [128, NCH, 512], F32, ph)
            sqb = [sb("sq%d" % i, [128, 512], BF16, ph) for i in range(2)]
            rstd = sb("rstd", [128, 512], F32, ph)
            ost = [sb("ost%d" % i, [128, 512], F32, ph) for i in range(4)]
            od = Dep()
            for th in range(2):
                for t2 in range(2):
                    c0 = th * 1024 + t2 * 512
                    pn = ps[6 + t2]
                    for c in range(NCH):
                        fw.dma(hbuf[:, c, :], HT[c * 128:(c + 1) * 128, c0:c0 + 512], reads=[HTd[c][th]], writes=[hbuf.d])
                        sq = sqb[c % 2]
                        act(sq[:], hbuf[:, c, :], AF.Square, [hbuf.d], [sq.d])
                        mm(pn[:], ones_bf[:], sq[:], c == 0, c == NCH - 1, [ones_bf.d, sq.d], [pn.d])
                    act(rstd[:], pn[:], AF.Sqrt, [pn.d, epsc.d], [rstd.d], scale=1.0 / D, bias=epsc[:])
                    recip(rstd[:], rstd[:], [rstd.d], [rstd.d])
                    for c in range(NCH):
                        o = ost[c % 4]
                        stt(o[:], hbuf[:, c, :], nwt[:, 4, c:c + 1], rstd[:], ALU.mult, ALU.mult, [hbuf.d, nwt.d, rstd.d], [o.d])
                        fw.dma(outT[c * 128:(c + 1) * 128, c0:c0 + 512], o[:], reads=[o.d], writes=[od])
            fw.barrier()


    def lru_side(ph):
        lcw = sb("lcw", [128, 8, 4], F32, ph)
        lv = sb("lv", [128, 4, 8], F32, ph)
        wa_f = sb("wa_f", [128, 8, 128], F32, ph)
        wx_f = sb("wx_f", [128, 8, 128], F32, ph)
        wa_r = sb("wa_r", [128, 8, 128], F32R, ph)
        wx_r = sb("wx_r", [128, 8, 128], F32R, ph)
        cst = sb("cst", [128, 8], F32, ph)
        cst2 = sb("cst2", [128, 8], F32, ph)
        fw.dma(lcw[:], lruconv, writes=[lcw.d])
        fw.dma(lv[:], lruvec, writes=[lv.d])
        fw.dma(wa_f[:], lruwa, writes=[wa_f.d])
        fw.dma(wx_f[:], lruwx, writes=[wx_f.d])
        cp(wa_r[:], wa_f[:], [wa_f.d], [wa_r.d], eng="act")
        cp(wx_r[:], wx_f[:], [wx_f.d], [wx_r.d], eng="act")
        act(cst[:], lv[:, 3, :], AF.Exp, [lv.d], [cst.d], scale=-1.0)
        act(cst[:], cst[:], AF.Ln, [cst.d, onec.d], [cst.d], bias=onec[:])
        ts(cst2[:], cst[:], -16.0, None, ALU.mult, None, [cst.d], [cst2.d])
        ts(cst[:], cst[:], -8.0, None, ALU.mult, None, [cst.d], [cst.d])
        B = [ps[6], ps[7], ps[6], ps[7]]
        xraw = sb("lxraw", [128, 3 + S], F32, ph)
        xc = sb("xc", [128, S], F32R, ph)
        yb = sb("yb", [128, S], F32, ph)
        r = sb("r", [128, S], F32, ph)
        ii = sb("ii", [128, S], F32, ph)
        aa = sb("aa", [128, S], F32, ph)
        t1 = sb("t1", [128, S], F32, ph)
        ho = sb("ho", [128, S], BF16, ph)
        fw.op("dve", lambda e: e.memset(xraw[:, 0:3], 0.0), [], [xraw.d])

        def lru(nb):
            dx = [PTf[(32 + nb, th_, t_)] for th_ in range(2) for t_ in range(2)]
            dy = [PTf[(40 + nb, th_, t_)] for th_ in range(2) for t_ in range(2)]
            fw.dma(xraw[:, 3:3 + S], PT[4096 + nb * 128:4096 + (nb + 1) * 128, :], reads=dx, writes=[xraw.d])
            fw.dma(yb[:], PT[5120 + nb * 128:5120 + (nb + 1) * 128, :], reads=dy, writes=[yb.d])
            ts(t1[:], xraw[:, 0:S], lcw[:, nb, 0:1], lv[:, 0, nb:nb + 1], ALU.mult, ALU.add, [xraw.d, lcw.d, lv.d], [t1.d])
            yield
            for j in range(1, 3):
                stt(t1[:], xraw[:, j:j + S], lcw[:, nb, j:j + 1], t1[:], ALU.mult, ALU.add, [xraw.d, lcw.d, t1.d], [t1.d])
                yield
            stt(xc[:], xraw[:, 3:3 + S], lcw[:, nb, 3:4], t1[:], ALU.mult, ALU.add, [xraw.d, lcw.d, t1.d], [xc.d])
            yield
            for t4 in range(4):
                sl = slice(t4 * 512, (t4 + 1) * 512)
                pa = B[(t4 % 2) * 2]
                pb = B[(t4 % 2) * 2 + 1]
                mm(pa[:], wa_r[:, nb, :], xc[:, sl], True, True, [wa_r.d, xc.d], [pa.d])
                mm(pb[:], wx_r[:, nb, :], xc[:, sl], True, True, [wx_r.d, xc.d], [pb.d])
                act(r[:, sl], pa[:], AF.Sigmoid, [pa.d, lv.d], [r.d], bias=lv[:, 1, nb:nb + 1])
                act(ii[:, sl], pb[:], AF.Sigmoid, [pb.d, lv.d], [ii.d], bias=lv[:, 2, nb:nb + 1])
                yield
            act(aa[:], r[:], AF.Exp, [r.d, cst.d], [aa.d], scale=cst[:, nb:nb + 1])
            act(t1[:], r[:], AF.Exp, [r.d, cst2.d], [t1.d], scale=cst2[:, nb:nb + 1])
            yield
            ts(t1[:], t1[:], -1.0, 1.0, ALU.mult, ALU.add, [t1.d], [t1.d])
            act(t1[:], t1[:], AF.Sqrt, [t1.d], [t1.d])
            yield
            tt(t1[:], t1[:], ii[:], ALU.mult, [t1.d, ii.d], [t1.d])
            yield
            tt(t1[:], t1[:], xc[:].bitcast(F32), ALU.mult, [t1.d, xc.d], [t1.d])
            yield
            fw.op("dve", lambda e: e.tensor_tensor_scan(out=r[:], data0=aa[:], data1=t1[:], initial=0.0, op0=ALU.mult,
                                                        op1=ALU.add), [aa.d, t1.d], [r.d])
            act(ii[:], yb[:], AF.Square, [yb.d], [ii.d])
            yield
            ts(ii[:], ii[:], 0.044715, 1.0, ALU.mult, ALU.add, [ii.d], [ii.d])
            yield
            tt(ii[:], ii[:], yb[:], ALU.mult, [ii.d, yb.d], [ii.d])
            act(ii[:], ii[:], AF.Sigmoid, [ii.d], [ii.d], scale=1.5957691216)
            yield
            tt(ii[:], ii[:], yb[:], ALU.mult, [ii.d, yb.d], [ii.d])
            yield
            tt(ho[:], ii[:], r[:], ALU.mult, [ii.d, r.d], [ho.d])
            fw.dma(OT[1024 + nb * 128:1024 + (nb + 1) * 128, :], ho[:], reads=[ho.d], writes=[Dep()])
            yield

        def stream():
            for nb in range(8):
                while (40 + nb, 1, 1) not in PTf or (32 + nb, 1, 1) not in PTf:
                    yield
                yield
                yield
                k = 0
                for _ in lru(nb):
                    k += 1
                    if k % 2 == 0:
                        yield
                yield

        return stream()

    def phase_mix0():
        with ExitStack() as ph:
            m64 = sb("m64", [64, 3, 512], F32, ph)
            selh = sb("selh", [8, 8, 128], F32, ph)
            gv = sb("gv", [8, 3], F32, ph)
            onwt = sb("onwt", [64, 128], F32, ph)
            cw = sb("cw", [128, 24, 4], F32, ph)
            betar = sb("betar", [8, S], F32, ph)
            gcr = sb("gcr", [8, S], F32, ph)
            gccol = sb("gccol", [64, 32, 8], F32, ph)
            bcol = sb("bcol", [64, 32, 8], F32, ph)
            egccol = sb("egccol", [64, 32, 8], F32, ph)
            Sts = [sb("St%d" % h, [128, 128], F32R, ph) for h in range(8)]
            fw.dma(m64[:], c_mask64, writes=[m64.d])
            fw.dma(selh[:], c_selh, writes=[selh.d])
            fw.dma(gv[:], gdnvec, writes=[gv.d])
            fw.dma(onwt[:], onw, writes=[onwt.d])
            fw.dma(cw[:], convqkv, writes=[cw.d])
            maskI = m64[:, 0, :]
            negS = m64[:, 1, :]
            idb = m64[:, 2, :]
            with ExitStack() as pre:
                reset = sb("reset", [8, S], F32, pre)
                braw = sb("braw", [8, S], F32, pre)
                araw = sb("araw", [8, S], F32, pre)
                negA = sb("negA", [8, 1], F32, pre)
                fw.dma(reset[:], c_reset, writes=[reset.d])
                fw.dma(braw[:], PT[6144:6152, :], reads=[PTd], writes=[braw.d])
                fw.dma(araw[:], PT[6152:6160, :], reads=[PTd], writes=[araw.d])
                act(betar[:], braw[:], AF.Sigmoid, [braw.d], [betar.d])
                act(araw[:], araw[:], AF.Exp, [araw.d, gv.d], [araw.d], bias=gv[:, 1:2])
                act(araw[:], araw[:], AF.Ln, [araw.d, onec.d], [araw.d], bias=onec[0:8, :])
                act(negA[:], gv[:, 0:1], AF.Exp, [gv.d], [negA.d])
                ts(negA[:], negA[:], -1.0, None, ALU.mult, None, [negA.d], [negA.d])
                ts(araw[:], araw[:], negA[:, 0:1], None, ALU.mult, None, [araw.d, negA.d], [araw.d])
                fw.op("dve", lambda e: e.tensor_tensor_scan(out=gcr[:], data0=reset[:], data1=araw[:], initial=0.0, op0=ALU.mult,
                                                            op1=ALU.add), [reset.d, araw.d], [gcr.d])
                for src, dst in ((gcr, gccol), (betar, bcol)):
                    p = ps[0]
                    for c in range(32):
                        tr(p[0:64, c * 8:(c + 1) * 8], src[:, c * 64:(c + 1) * 64], ident[0:8, 0:8], [src.d, ident.d], [p.d], inc=(c == 31))
                    cp(dst[:].rearrange("p a b -> p (a b)"), p[0:64, 0:256], [p.d], [dst.d])
                act(egccol[:], gccol[:], AF.Exp, [gccol.d], [egccol.d])
                for h in range(8):
                    ts(Sts[h][:], ident[:], 0.0, None, ALU.mult, None, [ident.d, Sts[h].d], [Sts[h].d])
                fw.barrier()

            def mkslot(g):
                n = NS()
                n.B = [ps[4 * g + i] for i in range(4)]
                n.raws = [sb("xraw%d" % i, [128, 3 + 512], F32, ph) for i in range(3)]
                n.loaded = None
                n.acc = sb("acc", [128, 512], F32, ph)
                n.vs = sb("vs", [128, 512], F32, ph)
                n.qT = sb("qT", [128, 512], F32R, ph)
                n.kT = sb("kT", [128, 512], F32R, ph)
                n.vtok = sb("vtok", [64, 8, 128], F32, ph)
                n.ktok = sb("ktok", [64, 8, 128], F32, ph)
                n.sqr = sb("sqr", [128, 512], F32R, ph)
                n.rn = sb("rn", [128, 512], F32, ph)
                n.gcb = sb("gcb", [128, 512], F32, ph)
                n.egcb = sb("egcb", [128, 512], F32, ph)
                n.betab = sb("betab", [64, 512], F32, ph)
                n.Dt = sb("Dt", [64, 512], F32, ph)
                n.tmpa = sb("tmpa", [64, 512], F32, ph)
                n.tmpb = sb("tmpb", [64, 512], F32, ph)
                n.tmpk = sb("tmpk", [64, 8, 128], F32, ph)
                n.attnT = sb("attnT", [64, 512], F32R, ph)
                n.A = [sb("A%d" % i, [64, 512], F32R, ph) for i in range(2)]
                n.At = [sb("At%d" % i, [64, 512], F32R, ph) for i in range(2)]
                n.X = sb("X", [64, 512], F32R, ph)
                n.vb = sb("vb", [64, 8, 128], F32R, ph)
                n.kbg = sb("kbg", [64, 8, 128], F32R, ph)
                n.kdec = sb("kdec", [64, 8, 128], F32R, ph)
                n.kds = sb("kds", [64, 8], F32, ph)
                n.u = sb("u", [64, 8, 128], F32, ph)
                n.wT = sb("wT", [128, 512], F32R, ph)
                n.qdT = sb("qdT", [128, 512], F32R, ph)
                n.vnew = sb("vnew", [64, 128], F32R, ph)
                n.ob = sb("ob", [64, 8, 128], F32, ph)
                n.oss = sb("oss", [64, 8], F32, ph)
                n.zr = sb("zr", [128, 512], F32, ph)
                n.oo = sb("oo", [128, 512], BF16, ph)
                return n

            slots = [mkslot(0), mkslot(1)]

            def gdn(h, b, n, nxt_hb=None):
                B0, B1, B2, B3 = n.B
                St = Sts[h]
                bs = slice(b * 512, (b + 1) * 512)
                acc, vs, qT, kT = n.acc, n.vs, n.qT, n.kT

                def issue_loads(h_, b_):
                    for i_, row0 in enumerate((h_ * 128, 1024 + h_ * 128, 2048 + h_ * 128)):
                        xr = n.raws[i_]
                        if b_ == 0:
                            fw.op("dve", lambda e: e.memset(xr[:, 0:3], 0.0), [], [xr.d])
                            fw.dma(xr[:, 3:515], PT[row0:row0 + 128, 0:512], reads=[PTd], writes=[xr.d])
                        else:
                            fw.dma(xr[:, 0:515], PT[row0:row0 + 128, b_ * 512 - 3:b_ * 512 + 512], reads=[PTd], writes=[xr.d])
                    n.loaded = (h_, b_)

                if n.loaded != (h, b):
                    issue_loads(h, b)
                fw.dma(n.zr[:], PT[3072 + h * 128:3072 + (h + 1) * 128, bs], reads=[PTd], writes=[n.zr.d])

                def conv_silu(which, wchunk):
                    xraw = n.raws[which]
                    ts(acc[:], xraw[:, 0:512], cw[:, wchunk, 0:1], None, ALU.mult, None, [xraw.d, cw.d], [acc.d])
                    for j in range(1, 4):
                        stt(acc[:], xraw[:, j:j + 512], cw[:, wchunk, j:j + 1], acc[:], ALU.mult, ALU.add, [xraw.d, cw.d, acc.d], [acc.d])
                    act(vs[:], acc[:], AF.Silu, [acc.d], [vs.d])

                def l2n(dst, extra_scale):
                    act(n.sqr[:], vs[:], AF.Square, [vs.d], [n.sqr.d])
                    mm(B0[:], ones_r[:], n.sqr[:], True, True, [ones_r.d, n.sqr.d], [B0.d])
                    act(n.rn[:], B0[:], AF.Ln, [B0.d, epsc.d], [n.rn.d], bias=epsc[:], scale=1.0)
                    act(n.rn[:], n.rn[:], AF.Exp, [n.rn.d], [n.rn.d], scale=-0.5)
                    stt(dst[:], vs[:], extra_scale, n.rn[:], ALU.mult, ALU.mult, [vs.d, n.rn.d], [dst.d])

                def to_tok(src_fn, src_dep, dst):
                    for g2 in range(2):
                        p = B1 if g2 == 0 else B2
                        for j in range(4):
                            c = g2 * 4 + j
                            tr(p[0:64, j * 128:(j + 1) * 128], src_fn(c), ident[:], [src_dep, ident.d], [p.d], inc=(j == 3))
                        cp(dst[:, g2 * 4:(g2 + 1) * 4, :].rearrange("p a b -> p (a b)"), p[0:64, :], [p.d], [dst.d],
                           eng=("act" if g2 else "dve"))

                conv_silu(0, h)
                if YS(1):
                    yield
                l2n(qT, SCALE)
                if YS(2):
                    yield
                conv_silu(1, 8 + h)
                if YS(3):
                    yield
                l2n(kT, 1.0)
                to_tok(lambda c: kT[:, c * 64:(c + 1) * 64].bitcast(F32), kT.d, n.ktok)
                if YS(4):
                    yield
                conv_silu(2, 16 + h)
                to_tok(lambda c: vs[:, c * 64:(c + 1) * 64], vs.d, n.vtok)
                if YS(5):
                    yield
                gcb, egcb, betab, Dt = n.gcb, n.egcb, n.betab, n.Dt
                mm(B0[:], selh[:, h, :], gcr[:, bs], True, True, [selh.d, gcr.d], [B0.d])
                cp(gcb[:], B0[:], [B0.d], [gcb.d])
                act(egcb[:], B0[:], AF.Exp, [B0.d], [egcb.d])
                mm(B1[:], selh[:, h, :], betar[:, bs], True, True, [selh.d, betar.d], [B1.d])
                cp(betab[:], B1[0:64, :], [B1.d], [betab.d], eng="act")
                if YS(6):
                    yield
                gcc = gccol[:, b * 8:(b + 1) * 8, h]
                bcc = bcol[:, b * 8:(b + 1) * 8, h]
                egc = egccol[:, b * 8:(b + 1) * 8, h]
                tt(Dt[:].rearrange("p (a b) -> p a b", b=64), gcb[0:64, :].rearrange("p (a b) -> p a b", b=64),
                   gcc.unsqueeze(2).broadcast_to([64, 8, 64]), ALU.subtract, [gcb.d, gccol.d], [Dt.d])
                ts(Dt[:], Dt[:], 0.0, None, ALU.min, None, [Dt.d], [Dt.d])
                act(Dt[:], Dt[:], AF.Exp, [Dt.d], [Dt.d])
                pk, pq = B2, B3
                for j in range(8):
                    cs = slice(j * 64, (j + 1) * 64)
                    mm(pk[0:64, cs], kT[:, cs], kT[:, cs], True, True, [kT.d], [pk.d], inc=(j == 7))
                for j in range(8):
                    cs = slice(j * 64, (j + 1) * 64)
                    mm(pq[0:64, cs], kT[:, cs], qT[:, cs], True, True, [kT.d, qT.d], [pq.d], inc=(j == 7))
                if YS(7):
                    yield
                tt(n.tmpa[:], pq[0:64, :], Dt[:], ALU.mult, [pq.d, Dt.d], [n.tmpa.d])
                tt(n.attnT[:], n.tmpa[:], maskI, ALU.mult, [n.tmpa.d, m64.d], [n.attnT.d], eng="pool")
                a0 = n.A[0]
                tt(n.tmpb[:], pk[0:64, :], Dt[:], ALU.mult, [pk.d, Dt.d], [n.tmpb.d])
                tt(n.tmpb[:], n.tmpb[:], betab[:], ALU.mult, [n.tmpb.d, betab.d], [n.tmpb.d], eng="pool")
                tt(a0[:], n.tmpb[:], negS, ALU.mult, [n.tmpb.d, m64.d], [a0.d], eng="pool")
                if YS(8):
                    yield
                for j in range(8):
                    tr(B0[0:64, j * 64:(j + 1) * 64], a0[:, j * 64:(j + 1) * 64].bitcast(F32), ident[0:64, 0:64], [a0.d, ident.d], [B0.d], inc=(j == 7))
                cp(n.At[0][:], B0[0:64, :], [B0.d], [n.At[0].d], eng="act")
                X = n.X
                tt(X[:], a0[:].bitcast(F32), idb, ALU.add, [a0.d, m64.d], [X.d], eng="pool")
                if YS(9):
                    yield
                cur = 0
                for l in range(1, 6):
                    nxt = 1 - cur
                    Ac, Atc, An, Atn = n.A[cur], n.At[cur], n.A[nxt], n.At[nxt]
                    pa, pb, pc = B1, B2, B3
                    for j in range(8):
                        js = slice(j * 64, (j + 1) * 64)
                        mm(pb[0:64, js], Ac[:, js], Atc[:, js], True, True, [Ac.d, Atc.d], [pb.d], inc=(j == 7))
                    cp(Atn[:], pb[0:64, :], [pb.d], [Atn.d], eng="act")
                    if l < 5:
                        for j in range(8):
                            js = slice(j * 64, (j + 1) * 64)
                            mm(pa[0:64, js], Atc[:, js], Ac[:, js], True, True, [Ac.d, Atc.d], [pa.d], inc=(j == 7))
                        cp(An[:], pa[0:64, :], [pa.d], [An.d], eng="dve")
                    if YS(10):
                        yield
                    for j in range(8):
                        js = slice(j * 64, (j + 1) * 64)
                        mm(pc[0:64, js], Atn[:, js], X[:, js], True, True, [Atn.d, X.d], [pc.d], inc=(j == 7))
                    tt(X[:], pc[0:64, :], X[:].bitcast(F32), ALU.add, [pc.d, X.d], [X.d])
                    cur = nxt
                    if YS(11):
                        yield
                vt, kt_ = n.vtok, n.ktok
                tt(n.vb[:], vt[:], bcc.unsqueeze(2).broadcast_to([64, 8, 128]), ALU.mult, [vt.d, bcol.d], [n.vb.d], eng="pool")
                tt(n.tmpk[:], kt_[:], bcc.unsqueeze(2).broadcast_to([64, 8, 128]), ALU.mult, [kt_.d, bcol.d], [n.tmpk.d], eng="pool")
                tt(n.kbg[:], n.tmpk[:], egc.unsqueeze(2).broadcast_to([64, 8, 128]), ALU.mult, [n.tmpk.d, egccol.d], [n.kbg.d], eng="pool")
                if YS(12):
                    yield
                tt(n.kds[:], gcb[0:64, 63::64], gcc, ALU.subtract, [gcb.d, gccol.d], [n.kds.d])
                act(n.kds[:], n.kds[:], AF.Exp, [n.kds.d], [n.kds.d])
                tt(n.kdec[:], kt_[:], n.kds[:].unsqueeze(2).broadcast_to([64, 8, 128]), ALU.mult, [kt_.d, n.kds.d], [n.kdec.d], eng="pool")
                if YS(13):
                    yield
                for half in range(2):
                    p = B0 if half == 0 else B1
                    for j4 in range(4):
                        j = half * 4 + j4
                        mm(p[0:64, j4 * 128:(j4 + 1) * 128], X[:, j * 64:(j + 1) * 64], n.vb[:, j, :], True, True, [X.d, n.vb.d], [p.d],
                           inc=(j4 == 3))
                    cp(n.u[:, half * 4:(half + 1) * 4, :].rearrange("p a b -> p (a b)"), p[0:64, :], [p.d], [n.u.d],
                       eng=("act" if half else "dve"))
                for j in range(8):
                    mm(B2[:, j * 64:(j + 1) * 64], n.kbg[:, j, :], X[:, j * 64:(j + 1) * 64], True, True, [n.kbg.d, X.d], [B2.d], inc=(j == 7))
                cp(n.wT[:], B2[:], [B2.d], [n.wT.d], eng="act")
                tt(n.qdT[:], qT[:].bitcast(F32), egcb[:], ALU.mult, [qT.d, egcb.d], [n.qdT.d], eng="pool")
                if YS(14):
                    yield
                if nxt_hb is not None:
                    issue_loads(*nxt_hb)
                for j in range(8):
                    js = slice(j * 64, (j + 1) * 64)
                    p1, p2, p3 = B3, B0, B1
                    mm(p1[0:64, 0:128], n.wT[:, js], St[:], True, True, [n.wT.d, St.d], [p1.d])
                    if YS(15):
                        yield
                    tt(n.vnew[:], n.u[:, j, :], p1[0:64, 0:128], ALU.subtract, [n.u.d, p1.d], [n.vnew.d])
                    if YS(16):
                        yield
                    mm(p2[0:64, 0:128], n.qdT[:, js], St[:], True, False, [n.qdT.d, St.d], [p2.d], inc=False)
                    mm(p2[0:64, 0:128], n.attnT[:, js], n.vnew[:], False, True, [n.attnT.d, n.vnew.d], [p2.d])
                    mm(p3[:, 0:128], n.kdec[:, j, :], n.vnew[:], True, True, [n.kdec.d, n.vnew.d], [p3.d])
                    if YS(17):
                        yield
                    stt(St[:], St[:].bitcast(F32), egcb[:, j * 64 + 63:j * 64 + 64], p3[:, 0:128], ALU.mult, ALU.add,
                        [St.d, egcb.d, p3.d], [St.d])
                    cp(n.ob[:, j, :], p2[0:64, 0:128], [p2.d], [n.ob.d], eng="act")
                    if YS(18):
                        yield
                ob, oss = n.ob, n.oss
                act(n.tmpk[:], ob[:], AF.Square, [ob.d], [n.tmpk.d])
                fw.op("dve", lambda e: e.tensor_reduce(out=oss[:], in_=n.tmpk[:], axis=AX.X, op=ALU.add), [n.tmpk.d], [oss.d])
                act(oss[:], oss[:], AF.Sqrt, [oss.d, epsc.d], [oss.d], scale=1.0 / 128.0, bias=epsc[0:64, :])
                recip(oss[:], oss[:], [oss.d], [oss.d])
                if YS(19):
                    yield
                tt(ob[:], ob[:], oss[:].unsqueeze(2).broadcast_to([64, 8, 128]), ALU.mult, [ob.d, oss.d], [ob.d])
                tt(ob[:], ob[:], onwt[:].unsqueeze(1).broadcast_to([64, 8, 128]), ALU.mult, [ob.d, onwt.d], [ob.d])
                for j in range(8):
                    tr(B2[:, j * 64:(j + 1) * 64], ob[:, j, :], ident[0:64, 0:64], [ob.d, ident.d], [B2.d], inc=(j == 7))
                act(n.zr[:], n.zr[:], AF.Silu, [n.zr.d], [n.zr.d])
                if YS(20):
                    yield
                tt(n.oo[:], B2[:], n.zr[:], ALU.mult, [B2.d, n.zr.d], [n.oo.d])
                fw.dma(OT[h * 128:(h + 1) * 128, bs], n.oo[:], reads=[n.oo.d], writes=[Dep()])
                if YS(21):
                    yield

            def gchain(g, delay):
                for _ in range(delay):
                    yield
                seq = [(2 * hp + g, b) for b in range(4) for hp in range(4)]
                for i_, (h_, b_) in enumerate(seq):
                    nxt = seq[i_ + 1] if i_ + 1 < len(seq) else None
                    for _ in gdn(h_, b_, slots[g], nxt):
                        yield

            run_rr([gchain(0, 0), gchain(1, GSTAG)], "gdn")
            fw.barrier()

    def phase_mix1():
        with ExitStack() as ph:
            trif = sb("trif", [128, 128], F32, ph)
            tri = sb("tri", [128, 128], BF16, ph)
            selh = sb("selh", [8, 8, 128], F32, ph)
            selpf = sb("selpf", [128, 8, 128], F32, ph)
            selr = sb("selr", [128, 8, 128], BF16, ph)
            gv = sb("gv", [8, 3], F32, ph)
            rope = sb("rope", [32, 2, S], F32, ph)
            mtab = sb("mtab", [128, 3, 16, 8], F32, ph)
            Fcol = sb("Fcol", [128, 16, 8], F32, ph)
            sel3f = sb("sel3f", [128, 8, 128], F32, ph)
            sel3 = sb("sel3", [128, 8, 128], BF16, ph)
            F3 = sb("F3", [128, S], BF16, ph)
            fw.dma(sel3f[:], c_sel3, writes=[sel3f.d])
            cp(sel3[:], sel3f[:], [sel3f.d], [sel3.d], eng="act")
            fw.dma(trif[:], c_tri, writes=[trif.d])
            fw.dma(selh[:], c_selh, writes=[selh.d])
            fw.dma(selpf[:], c_selp, writes=[selpf.d])
            fw.dma(gv[:], gdnvec, writes=[gv.d])
            fw.dma(rope[:], c_rope, writes=[rope.d])
            fw.dma(mtab[:], c_moba, writes=[mtab.d])
            cp(selr[:], selpf[:], [selpf.d], [selr.d], eng="act")
            ts(tri[:], trif[:], -1.0, -NEG / SCALE, ALU.add, ALU.mult, [trif.d], [tri.d])
            identb = sb("identb", [128, 128], BF16, ph)
            cp(identb[:], ident[:], [ident.d], [identb.d], eng="act")
            with ExitStack() as pre:
                onesrow = sb("onesrow", [8, S], F32, pre)
                ffr = sb("ffr", [8, S], F32, pre)
                Fr = sb("Fr", [8, S], F32, pre)
                negfb = sb("negfb", [8, 1], F32, pre)
                fw.op("dve", lambda e: e.memset(onesrow[:], 1.0), [], [onesrow.d])
                fw.dma(ffr[:], PT[6144:6152, :], reads=[PTd], writes=[ffr.d])
                ts(negfb[:], gv[:, 2:3], -1.0, None, ALU.mult, None, [gv.d], [negfb.d])
                act(ffr[:], ffr[:], AF.Exp, [ffr.d, negfb.d], [ffr.d], scale=-1.0, bias=negfb[:, 0:1])
                act(ffr[:], ffr[:], AF.Ln, [ffr.d, onec.d], [ffr.d], bias=onec[0:8, :])
                ts(ffr[:], ffr[:], -1.0, None, ALU.mult, None, [ffr.d], [ffr.d])
                fw.op("dve", lambda e: e.tensor_tensor_scan(out=Fr[:], data0=onesrow[:], data1=ffr[:], initial=0.0, op0=ALU.mult,
                                                            op1=ALU.add), [onesrow.d, ffr.d], [Fr.d])
                p = ps[0]
                for kt in range(16):
                    tr(p[:, kt * 8:(kt + 1) * 8], Fr[:, kt * 128:(kt + 1) * 128], ident[0:8, 0:8], [Fr.d, ident.d], [p.d], inc=(kt == 15))
                cp(Fcol[:].rearrange("p a b -> p (a b)"), p[:, 0:128], [p.d], [Fcol.d])
                ts(Fcol[:], Fcol[:], -1.0, None, ALU.mult, None, [Fcol.d], [Fcol.d])
                F3f = sb("F3f", [128, S], F32, pre)
                hb = sb("hb", [8, S], BF16, pre)
                hf = sb("hf", [8, S], F32, pre)
                mf = sb("mf", [8, S], F32, pre)
                fw.op("dve", lambda e: e.memset(F3f[:], 0.0), [], [F3f.d])
                ts(Fr[:], Fr[:], 1.0 / SCALE, None, ALU.mult, None, [Fr.d], [Fr.d])
                cp(hb[:], Fr[:], [Fr.d], [hb.d])
                cp(hf[:], hb[:], [hb.d], [hf.d])
                tt(Fr[:], Fr[:], hf[:], ALU.subtract, [Fr.d, hf.d], [Fr.d])
                cp(hb[:], Fr[:], [Fr.d], [hb.d])
                cp(mf[:], hb[:], [hb.d], [mf.d])
                tt(Fr[:], Fr[:], mf[:], ALU.subtract, [Fr.d, mf.d], [Fr.d])
                fw.dma(F3f[0:8, :], hf[:], reads=[hf.d], writes=[F3f.d])
                fw.dma(F3f[32:40, :], mf[:], reads=[mf.d], writes=[F3f.d])
                fw.dma(F3f[64:72, :], Fr[:], reads=[Fr.d], writes=[F3f.d])
                cp(F3[:], F3f[:], [F3f.d], [F3.d], eng="act")
                fw.barrier()

            def mks(g):
                n = NS()
                n.B = [ps[4 * g + i] for i in range(4)]
                n.xq = sb("xq", [128, S], F32, ph)
                n.xk = sb("xk", [128, S], F32, ph)
                n.xv = sb("xv", [128, S], F32, ph)
                n.xsw = sb("xsw", [32, S], F32, ph)
                n.qT = sb("qT", [128, S], BF16, ph)
                n.kT = sb("kT", [128, S], BF16, ph)
                n.vtok = sb("vtok", [128, 16, 128], BF16, ph)
                n.pt = [sb("pt%d" % i, [128, 512], BF16, ph) for i in range(4)]
                n.rec = sb("rec", [128, 512], F32, ph)
                n.oo = [sb("oo%d" % i, [128, 512], BF16, ph) for i in range(2)]
                n.kmean = sb("kmean", [128, 8], F32, ph)
                n.gm = sb("gm", [128, 16, 8], F32, ph)
                n.top8 = sb("top8", [128, 16, 8], F32, ph)
                n.nm = sb("nm", [128, 16, 8], F32, ph)
                n.nmT = sb("nmT", [128, S], BF16, ph)
                fw.op("dve", lambda e: e.memset(n.nmT[:], 0.0), [], [n.nmT.d])
                return n

            aslots = [mks(0), mks(1)]

            def attn(hh, n):
                moba = hh >= 8
                h = hh % 8
                base = 3072 if moba else 0
                B0, B1, B2, B3 = n.B
                xq, xk, xv, xsw, qT, kT, vtok = n.xq, n.xk, n.xv, n.xsw, n.qT, n.kT, n.vtok
                fw.dma(xq[:], PT[base + h * 128:base + (h + 1) * 128, :], reads=[PTd], writes=[xq.d])
                fw.dma(xk[:], PT[base + 1024 + h * 128:base + 1024 + (h + 1) * 128, :], reads=[PTd], writes=[xk.d])
                fw.dma(xv[:], PT[base + 2048 + h * 128:base + 2048 + (h + 1) * 128, :], reads=[PTd], writes=[xv.d])
                if moba:
                    for row0, dst in ((base + h * 128, xq), (base + 1024 + h * 128, xk)):
                        fw.dma(xsw[0:16, :], PT[row0 + 16:row0 + 32, :], reads=[PTd], writes=[xsw.d])
                        fw.dma(xsw[16:32, :], PT[row0:row0 + 16, :], reads=[PTd], writes=[xsw.d])
                        tt(dst[0:32, :], dst[0:32, :], rope[:, 0, :], ALU.mult, [dst.d, rope.d], [dst.d])
                        tt(xsw[:], xsw[:], rope[:, 1, :], ALU.mult, [xsw.d, rope.d], [xsw.d])
                        tt(dst[0:32, :], dst[0:32, :], xsw[:], ALU.add, [dst.d, xsw.d], [dst.d])
                        yield
                cp(qT[:], xq[:], [xq.d], [qT.d], eng="act")
                cp(kT[:], xk[:], [xk.d], [kT.d], eng="dve")
                yield
                for g4 in range(4):
                    p = B0 if g4 % 2 == 0 else B1
                    for j in range(4):
                        kt = g4 * 4 + j
                        tr(p[:, j * 128:(j + 1) * 128], xv[:, kt * 128:(kt + 1) * 128], ident[:], [xv.d, ident.d], [p.d], inc=(j == 3))
                    cp(vtok[:, g4 * 4:(g4 + 1) * 4, :].rearrange("p a b -> p (a b)"), p[:], [p.d], [vtok.d],
                       eng=("act" if g4 % 2 else "dve"))
                    yield
                if moba:
                    kmean, gm, top8, nm, nmT = n.kmean, n.gm, n.top8, n.nm, n.nmT
                    fw.op("dve", lambda e: e.tensor_reduce(out=kmean[:], in_=xk[:].rearrange("p (a b) -> p a b", b=256), axis=AX.X,
                                                           op=ALU.add), [xk.d], [kmean.d])
                    ts(kmean[:], kmean[:], 1.0 / 256.0, None, ALU.mult, None, [kmean.d], [kmean.d])
                    for qt in range(16):
                        mm(B2[:, qt * 8:(qt + 1) * 8], xq[:, qt * 128:(qt + 1) * 128], kmean[:], True, True, [xq.d, kmean.d], [B2.d], inc=(qt == 15))
                    yield
                    tt(gm[:].rearrange("p a b -> p (a b)"), B2[:, 0:128], mtab[:, 0, :, :].rearrange("p a b -> p (a b)"), ALU.add,
                       [B2.d, mtab.d], [gm.d])
                    for qt in range(16):
                        fw.op("dve", lambda e: e.max(out=top8[:, qt, :], in_=gm[:, qt, :]), [gm.d], [top8.d])
                    yield
                    tt(nm[:], gm[:], top8[:, :, 2:3].broadcast_to([128, 16, 8]), ALU.is_ge, [gm.d, top8.d], [nm.d])
                    tt(nm[:], nm[:], mtab[:, 1, :, :], ALU.mult, [nm.d, mtab.d], [nm.d])
                    tt(nm[:], nm[:], mtab[:, 2, :, :], ALU.add, [nm.d, mtab.d], [nm.d])
                    ts(nm[:], nm[:], -NEG / SCALE, NEG / SCALE, ALU.mult, ALU.add, [nm.d], [nm.d])
                    yield
                    for g4 in range(4):
                        for j in range(4):
                            qt = g4 * 4 + j
                            tr(B3[0:8, j * 128:(j + 1) * 128], nm[:, qt, :], ident[:], [nm.d, ident.d], [B3.d], inc=(j == 3))
                        cp(nmT[0:8, g4 * 512:(g4 + 1) * 512], B3[0:8, :], [B3.d], [nmT.d], eng="act")
                        yield
                pO, pS = B2, B3
                steps = [(qi, kt) for qi in range(4) for kt in range(4 * qi + 4)]

                def geo(i):
                    qi, kt = steps[i]
                    r_ = kt - 4 * qi
                    return qi, kt, r_, max(r_, 0) * 128, qi * 512

                def qk(i):
                    qi, kt, r_, c0, q0 = geo(i)
                    pq = B0 if i % 2 == 0 else B1
                    mm(pq[:, c0:512], kT[:, kt * 128:(kt + 1) * 128], qT[:, q0 + c0:q0 + 512], True, False, [kT.d, qT.d], [pq.d],
                       inc=False)
                    diag = r_ >= 0
                    if moba:
                        mm(pq[:, c0:512], selr[:, kt // 2, :], n.nmT[:, q0 + c0:q0 + 512], False, not diag, [selr.d, n.nmT.d], [pq.d],
                           inc=(not diag))
                    else:
                        mm(pq[:, c0:512], sel3[:, h, :], F3[:, q0 + c0:q0 + 512], False, not diag, [sel3.d, F3.d], [pq.d], inc=(not diag))
                    if diag:
                        mm(pq[:, c0:c0 + 128], identb[:], tri[:], False, True, [identb.d, tri.d], [pq.d])

                qk(0)
                yield
                for i in range(len(steps)):
                    qi, kt, r_, c0, q0 = geo(i)
                    nkt = 4 * qi + 4
                    pq = B0 if i % 2 == 0 else B1
                    P = n.pt[i % 4]
                    if i + 1 < len(steps):
                        qk(i + 1)
                        yield
                    if moba:
                        act(P[:, c0:512], pq[:, c0:512], AF.Exp, [pq.d], [P.d], scale=SCALE)
                    else:
                        act(P[:, c0:512], pq[:, c0:512], AF.Exp, [pq.d, Fcol.d], [P.d], scale=SCALE, bias=Fcol[:, kt, h:h + 1])
                    yield
                    mm(pO[:, c0:512], vtok[:, kt, :], P[:, c0:512], kt == 0, kt == nkt - 1, [vtok.d, P.d], [pO.d], inc=False)
                    mm(pS[:, c0:512], ones_bf[:], P[:, c0:512], kt == 0, kt == nkt - 1, [ones_bf.d, P.d], [pS.d])
                    if kt == nkt - 1:
                        recip(n.rec[:], pS[:], [pS.d], [n.rec.d])
                        o = n.oo[qi % 2]
                        tt(o[:], pO[:], n.rec[:], ALU.mult, [pO.d, n.rec.d], [o.d])
                        fw.dma(OT[hh * 128:(hh + 1) * 128, qi * 512:(qi + 1) * 512], o[:], reads=[o.d], writes=[Dep()])
                    yield

            def achain(g, delay):
                for _ in range(delay):
                    yield
                for hp in range(8):
                    for _ in attn(2 * hp + g, aslots[g]):
                        yield

            run_rr([achain(0, 0), achain(1, ASTAG)], "attn")
            fw.barrier()


    fw.barrier()
    stages = [
        ("in0", lambda: phase_inproj(w_in0, F0, 0, src=xT, order1=([c_ for n_ in range(8) for c_ in (32 + n_, 40 + n_)] + [48] + list(range(0, 32))), side=(None if lim is not None else lru_side))),
        ("mix0", phase_mix0),
        ("ffn0", lambda: phase_ffn(0, 1, Wout=w_out0, res_src=xT)),
        ("in1", lambda: phase_inproj(w_in1, F1, 2)),
        ("mix1", phase_mix1),
        ("ffn1", lambda: phase_ffn(1, 3, Wout=w_out1, res_src=HT, fin=True)),
    ]
    for name, fn in stages:
        fn()
        if stop_after == name:
            break
    fw.barrier()
    es.close()
    return nc


def _consts():
    c = {}
    c["c_ident"] = np.eye(128, dtype=np.float32)
    s = np.arange(64)[:, None]
    cc = np.arange(64)[None, :]
    mI = (s <= cc).astype(np.float32)
    mS = -(s < cc).astype(np.float32)
    I64 = np.eye(64, dtype=np.float32)
    c["c_mask64"] = np.ascontiguousarray(np.stack([np.tile(mI, (1, 8)), np.tile(mS, (1, 8)), np.tile(I64, (1, 8))], axis=1))
    k = np.arange(128)[:, None]
    q = np.arange(128)[None, :]
    c["c_tri"] = (k <= q).astype(np.float32)
    r = np.ones((8, S), np.float32)
    r[:, ::64] = 0.0
    c["c_reset"] = r
    sel = np.zeros((8, 8, 128), np.float32)
    for h in range(8):
        sel[h, h, :] = 1.0
    c["c_selh"] = sel
    selp = np.zeros((128, 8, 128), np.float32)
    selp[0:8] = sel
    c["c_selp"] = selp
    sel3 = np.zeros((128, 8, 128), np.float32)
    for h in range(8):
        sel3[h, h, :] = 1.0
        sel3[32 + h, h, :] = 1.0
        sel3[64 + h, h, :] = 1.0
    c["c_sel3"] = sel3
    half = 16
    inv = (500000.0 ** (-np.arange(half, dtype=np.float32) / half)).astype(np.float32)
    ang = (np.arange(S, dtype=np.float32)[None, :] * inv[:, None]).astype(np.float32)
    cos = np.cos(ang.astype(np.float64)).astype(np.float32)
    sin = np.sin(ang.astype(np.float64)).astype(np.float32)
    rope = np.zeros((32, 2, S), np.float32)
    rope[0:16, 0] = cos
    rope[16:32, 0] = cos
    rope[0:16, 1] = -sin
    rope[16:32, 1] = sin
    c["c_rope"] = rope
    mt = np.zeros((128, 3, 16, 8), np.float32)
    for qt in range(16):
        b = qt // 2
        for j in range(8):
            mt[:, 0, qt, j] = 0.0 if j < b else -1e30
            mt[:, 1, qt, j] = 1.0 if j < b else 0.0
            mt[:, 2, qt, j] = 1.0 if j == b else 0.0
    c["c_moba"] = mt
    return c


def _prep_weights(inp):
    f = lambda a: np.ascontiguousarray(np.asarray(a, dtype=np.float32))
    w = {}
    wi0 = inp["ab_w_in"][0]
    w["w_in0"] = f(np.concatenate([wi0[:, 0:4096], wi0[:, 4112:6160], wi0[:, 4096:4112]], axis=1))
    wi1 = inp["cd_w_in"][0]
    w["w_in1"] = f(np.concatenate([wi1[:, 0:3072], wi1[:, 3080:6152], wi1[:, 3072:3080]], axis=1))
    w["w_out0"] = f(inp["ab_w_out"][0])
    w["w_out1"] = f(inp["cd_w_out"][0])
    w["w_gate"] = f(inp["ffn_w_gate"])
    w["w_up"] = f(inp["ffn_w_up"])
    w["w_down"] = f(inp["ffn_w_down"])
    nw = np.stack([inp["ab_norm"][0], inp["ffn_norm"][0], inp["cd_norm"][0], inp["ffn_norm"][1], inp["final_norm"]], axis=0)
    w["normw"] = f(nw.reshape(5, NCH, 128).transpose(2, 0, 1))
    w["convqkv"] = f(inp["ab_conv_qkv"][0].T.reshape(24, 128, 4).transpose(1, 0, 2))
    w["lruconv"] = f(inp["ab_lru_conv_w"][0].T.reshape(8, 128, 4).transpose(1, 0, 2))
    lv = np.stack([inp["ab_lru_conv_b"][0], inp["ab_lru_ba"][0], inp["ab_lru_bx"][0], inp["ab_lru_lambda"][0]], axis=0)
    w["lruvec"] = f(lv.reshape(4, 8, 128).transpose(2, 0, 1))
    w["lruwa"] = f(np.asarray(inp["ab_lru_wa"][0]).transpose(1, 0, 2))
    w["lruwx"] = f(np.asarray(inp["ab_lru_wx"][0]).transpose(1, 0, 2))
    w["gdnvec"] = f(np.stack([inp["ab_a_log"][0], inp["ab_dt_bias"][0], inp["cd_f_bias"][0]], axis=1))
    w["onw"] = f(np.broadcast_to(np.asarray(inp["ab_out_norm"][0])[None, :], (64, 128)))
    w.update(_consts())
    return w


def kernel(**inputs):
    inp = {k: np.asarray(v) for k, v in inputs.items()}
    x = inp["x"]
    B = x.shape[0]
    w = _prep_weights(inp)
    nc = build()
    in_maps = []
    for b in range(B):
        m = dict(w)
        m["xT"] = np.ascontiguousarray(x[b].T.astype(np.float32))
        in_maps.append(m)
    res = run_bass_kernel_spmd(nc, in_maps, core_ids=list(range(B)))
    out = np.stack([np.ascontiguousarray(r["outT"].T) for r in res.results], axis=0)
    return out.astype(np.float32)
```
